# Optimizing a Trainium2 kernel written in Bass

```python
import jax, jax.numpy as jnp
from jax import lax
import numpy as np

D_MODEL = 4096
BATCH = 4
SEQ = 2048
DEPTH = 1
DEC_BATCH = 32
DEC_SEQ = 64
PAST_LEN = 1024

CHUNK = 64
N_HEADS = 16
HEAD_DIM = 128
W_ATT = N_HEADS * HEAD_DIM
W_CONV = D_MODEL // 2
CONV_WIDTH = 3
Q_BLOCK = 128
DN_ALPHA = (2.0 * DEPTH) ** 0.25
DN_BETA = (8.0 * DEPTH) ** -0.25
LN_EPS = 1e-5

kernel_name = 'stickbreak_shortconv_gated_hybrid_step'


def _split_sizes():
    sizes = [W_ATT] * 4 + [W_CONV] * 4 + [D_MODEL] * 2
    return [int(o) for o in np.cumsum(sizes)[:-1]]


def _layer_norm(x, g, b):
    xf = x.astype(jnp.float32)
    mu = jnp.mean(xf, axis=-1, keepdims=True)
    var = jnp.mean(jnp.square(xf - mu), axis=-1, keepdims=True)
    y = (xf - mu) * lax.rsqrt(var + LN_EPS) * g.astype(jnp.float32) + b.astype(jnp.float32)
    return y.astype(x.dtype)


def _stick_breaking(q, k, v, q_pos, k_pos):
    z = jnp.einsum('bqhd,bkhd->bhqk', q.astype(jnp.float32), k.astype(jnp.float32)) * (HEAD_DIM ** -0.5)
    causal = k_pos[None, :] < q_pos[:, None]
    log_keep = jnp.where(causal, jax.nn.log_sigmoid(-z), 0.0)
    after = lax.cumsum(log_keep, axis=3, reverse=True) - log_keep
    w = jnp.where(causal, jnp.exp(jax.nn.log_sigmoid(z) + after), 0.0)
    o = jnp.einsum('bhqk,bkhd->bqhd', w, v.astype(jnp.float32))
    return o.astype(v.dtype)


def _layer(x, past_k, past_v, past_conv, w_in, b_in, conv_w, conv_b, w_a, w_b, w_o, ln_g, ln_b):
    bsz, t_len, _ = x.shape
    proj = jnp.einsum('bsd,de->bse', x, w_in) + b_in
    q, k, v, z_a, b_g, c_g, h, z_b, g_a, g_b = jnp.split(proj, _split_sizes(), axis=-1)
    q = q.reshape(bsz, t_len, N_HEADS, HEAD_DIM)
    k = k.reshape(bsz, t_len, N_HEADS, HEAD_DIM)
    v = v.reshape(bsz, t_len, N_HEADS, HEAD_DIM)

    if past_k is None:
        blocks = []
        for i in range(t_len // Q_BLOCK):
            t0, t1 = i * Q_BLOCK, (i + 1) * Q_BLOCK
            blocks.append(_stick_breaking(q[:, t0:t1], k[:, :t1], v[:, :t1],
                                          jnp.arange(t0, t1), jnp.arange(t1)))
        o = jnp.concatenate(blocks, axis=1)
    else:
        p_len = past_k.shape[1]
        k_all = jnp.concatenate([past_k, k], axis=1)
        v_all = jnp.concatenate([past_v, v], axis=1)
        o = _stick_breaking(q, k_all, v_all, p_len + jnp.arange(t_len), jnp.arange(p_len + t_len))
    y_a = o.reshape(bsz, t_len, W_ATT) * jax.nn.silu(z_a)
    p_a = jnp.einsum('bsc,cd->bsd', y_a, w_a)

    u = c_g * h
    if past_conv is None:
        past_conv = jnp.zeros((bsz, CONV_WIDTH - 1, W_CONV), u.dtype)
    full = jnp.concatenate([past_conv, u], axis=1)
    conv = conv_b + sum(full[:, i:i + t_len] * conv_w[i] for i in range(CONV_WIDTH))
    new_conv = full[:, -(CONV_WIDTH - 1):]
    y_b = jax.nn.silu(z_b) * b_g * conv
    p_b = jnp.einsum('bsc,cd->bsd', y_b, w_b)

    merged = jax.nn.sigmoid(g_a) * p_a + jax.nn.sigmoid(g_b) * p_b
    sub = jnp.einsum('bsd,de->bse', merged, w_o)
    y = _layer_norm(DN_ALPHA * x + sub, ln_g, ln_b)
    return y, k, v, new_conv


def setup_inputs(seed: int = 0) -> dict:
    key = jax.random.key(seed)
    ks = jax.random.split(key, 14)
    n_in = 4 * W_ATT + 4 * W_CONV + 2 * D_MODEL
    f32 = jnp.float32
    return {
        'x_prompt': jax.random.normal(ks[0], (BATCH, SEQ, D_MODEL), f32),
        'x_sample': jax.random.normal(ks[1], (DEC_BATCH, DEC_SEQ, D_MODEL), f32),
        'cache_k': jax.random.normal(ks[2], (DEPTH, DEC_BATCH, PAST_LEN, N_HEADS, HEAD_DIM), f32),
        'cache_v': jax.random.normal(ks[3], (DEPTH, DEC_BATCH, PAST_LEN, N_HEADS, HEAD_DIM), f32),
        'state_conv': jax.random.normal(ks[4], (DEPTH, DEC_BATCH, CONV_WIDTH - 1, W_CONV), f32),
        'w_in': jax.random.normal(ks[5], (DEPTH, D_MODEL, n_in), f32) * D_MODEL ** -0.5,
        'b_in': jax.random.normal(ks[6], (DEPTH, n_in), f32) * 0.02,
        'conv_w': jax.random.normal(ks[7], (DEPTH, CONV_WIDTH, W_CONV), f32) * CONV_WIDTH ** -0.5,
        'conv_b': jax.random.normal(ks[8], (DEPTH, W_CONV), f32) * 0.02,
        'w_a': jax.random.normal(ks[9], (DEPTH, W_ATT, D_MODEL), f32) * (W_ATT ** -0.5 * DN_BETA),
        'w_b': jax.random.normal(ks[10], (DEPTH, W_CONV, D_MODEL), f32) * (W_CONV ** -0.5 * DN_BETA),
        'w_o': jax.random.normal(ks[11], (DEPTH, D_MODEL, D_MODEL), f32) * (D_MODEL ** -0.5 * DN_BETA),
        'ln_g': 1.0 + 0.02 * jax.random.normal(ks[12], (DEPTH, D_MODEL), f32),
        'ln_b': 0.02 * jax.random.normal(ks[13], (DEPTH, D_MODEL), f32),
    }


def reference(x_prompt, x_sample, cache_k, cache_v, state_conv, w_in, b_in, conv_w, conv_b,
              w_a, w_b, w_o, ln_g, ln_b):
    xp, xs = x_prompt, x_sample
    kp, vp, cp, kn, vn, cn = [], [], [], [], [], []
    for l in range(DEPTH):
        wts = (w_in[l], b_in[l], conv_w[l], conv_b[l], w_a[l], w_b[l], w_o[l], ln_g[l], ln_b[l])
        xp, k1, v1, c1 = _layer(xp, None, None, None, *wts)
        xs, k2, v2, c2 = _layer(xs, cache_k[l], cache_v[l], state_conv[l], *wts)
        kp.append(k1); vp.append(v1); cp.append(c1)
        kn.append(k2); vn.append(v2); cn.append(c2)
    return (xp, xs, jnp.stack(kp), jnp.stack(vp), jnp.stack(cp), jnp.stack(kn), jnp.stack(vn), jnp.stack(cn))
```

```python
import contextlib
import numpy as np
import concourse.bass as bass
import concourse.mybir as mybir
from concourse.bass_utils import run_bass_kernel_spmd

F32 = mybir.dt.float32
BF16 = mybir.dt.bfloat16
AF = mybir.ActivationFunctionType
ALU = mybir.AluOpType

D = 4096
NH = 16
HD = 128
NIN = 24576
NT = 1280
NTX = 1282
NHIST = 1024
NPR = 1024
NSQ = 4
SQ = 64
PAST = 1024
DN_ALPHA = 2.0 ** 0.25
LN_EPS = 1e-5
QSCALE = HD ** -0.5
TT3 = [(0, 512), (512, 512), (1024, 258)]
TT3o = [(0, 512), (512, 512), (1024, 256)]

C_ID, C_TRI, C_ONE, C_M01, C_MB, C_SM01, C_SMB, C_END = 0, 128, 256, 384, 2432, 4480, 4992, 5504


class Sem:
    def __init__(self, h):
        self.h = h
        self.cnt = 0


class Prog:
    ENGS = ("tensor", "scalar", "vector", "gpsimd", "sync")

    def __init__(self, nc, esems, dpool):
        self.nc = nc
        self.es = esems
        self.dpool = dpool
        self.used = []
        self.q = {e: [] for e in self.ENGS}
        self.dtoks = {}

    def dsem(self):
        s = self.dpool.pop()
        self.used.append(s)
        return s

    def op(self, eng, fn, waits=(), signal=True):
        tok = None
        if signal:
            s = self.es[eng]
            s.cnt += 1
            tok = (s, s.cnt)
        self.q[eng].append(([w for w in waits if w is not None], fn, tok, 1))
        return tok

    def dma(self, eng, out, in_, sem, waits=()):
        sem.cnt += 16
        tok = (sem, sem.cnt)
        self.dtoks[id(sem)] = tok
        self.q[eng].append(([w for w in waits if w is not None],
                            (lambda e, o=out, i=in_: e.dma_start(out=o, in_=i)), tok, 16))
        return tok

    def check_deadlock(self):
        cnt = getattr(self, "_simcnt", {})
        ptr = {e: 0 for e in self.ENGS}
        progress = True
        while progress:
            progress = False
            for e in self.ENGS:
                q = self.q[e]
                while ptr[e] < len(q):
                    waits, fn, tok, inc = q[ptr[e]]
                    if all(cnt.get(id(s_), 0) >= v for (s_, v) in waits):
                        if tok is not None:
                            cnt[id(tok[0])] = cnt.get(id(tok[0]), 0) + inc
                        ptr[e] += 1
                        progress = True
                    else:
                        break
        stuck = {e: ptr[e] for e in self.ENGS if ptr[e] < len(self.q[e])}
        self._simcnt = cnt
        if stuck:
            msg = []
            for e, p in stuck.items():
                waits, fn, tok, inc = self.q[e][p]
                msg.append("%s@%d/%d waits %s" % (e, p, len(self.q[e]),
                           [(next((k for k, v_ in self.es.items() if v_ is s_), "dsem"), v, cnt.get(id(s_), 0)) for (s_, v) in waits]))
            raise RuntimeError("semaphore plan deadlocks: " + "; ".join(msg))

    def run_block(self):
        self.q["sync"].append((list(self.dtoks.values()), None, None, 0))
        self.check_deadlock()
        nc = self.nc
        qs = self.q

        def mk(name):
            def body(e):
                waited = {}
                for waits, fn, tok, inc in qs[name]:
                    for (s, v) in waits:
                        if waited.get(id(s), 0) < v:
                            e.wait_ge(s.h, v)
                            waited[id(s)] = v
                    if fn is None:
                        continue
                    ins = fn(e)
                    if tok is not None:
                        ins.then_inc(tok[0].h, inc)
            return body

        with nc.Block() as block:
            block.tensor(mk("tensor"))
            block.scalar(mk("scalar"))
            block.vector(mk("vector"))
            block.gpsimd(mk("gpsimd"))
            block.sync(mk("sync"))
        self.q = {e: [] for e in self.ENGS}
        self.dtoks = {}
        self.dpool.extend(self.used)
        self.used = []


def build_program(debug=False):
    nc = bass.Bass("TRN2", target_bir_lowering=False)

    def din(name, shape, dt=F32):
        return nc.dram_tensor(name, list(shape), dt, kind="ExternalInput").ap()

    def dout(name, shape, dt=F32):
        return nc.dram_tensor(name, list(shape), dt, kind="ExternalOutput").ap()

    def dscr(name, shape, dt):
        return nc.dram_tensor(name, list(shape), dt, kind="ExternalOutput" if debug else "Internal").ap()

    xT_own = din("xT_own", [128, 32, NTX])
    xT_hist = din("xT_hist", [128, 32, NHIST])
    x_tok = din("x_tok", [NT, D])
    w_in = din("w_in", [D, NIN])
    w_a = din("w_a", [2048, D])
    w_b = din("w_b", [2048, D])
    w_o = din("w_o", [D, D])
    bias_fm = din("bias_fm", [128, 192])
    bv_bc = din("bv_bc", [128, 2048])
    cw_fm = din("cw_fm", [128, 16 * 3])
    cb_fm = din("cb_fm", [128, 16])
    sc_fm = din("sc_fm", [128, 16 * 8])
    lng_bc = din("lng_bc", [128, D])
    lnb_bc = din("lnb_bc", [128, D])
    flag_in = din("flag", [128, 1])
    consts_in = din("consts", [128, C_END])
    ckT_in = din("ckT", [NSQ, NH, 128, PAST])
    cv_in = din("cv", [NSQ, PAST, 2048])

    y_out = dout("y_out", [NT, D])
    k_out = dout("k_out", [NT, 2048])
    v_out = dout("v_out", [NT, 2048])
    conv_out = dout("conv_out", [128, 160])

    qT_s = dscr("qT_s", [NH, 128, NT], BF16)
    kT_s = dscr("kT_s", [NH, 128, NT], BF16)
    kTh_s = dscr("kTh_s", [NH, 128, NHIST], BF16)
    v_s = dscr("v_s", [NT, 2048], BF16)
    vh_s = dscr("vh_s", [NHIST, 2048], BF16)
    sza_s = dscr("sza_s", [NH, 128, NT], F32)
    yb_s = dscr("yb_s", [16, 128, NT], BF16)
    sg_s = dscr("sg_s", [64, 128, NT], F32)
    ya_s = dscr("ya_s", [NH, 128, NT], BF16)
    mg_s = dscr("mg_s", [32, 128, NT], BF16)
    h_s = dscr("h_s", [NT, D], F32)

    w_in_v = w_in.rearrange("(c p) e -> p c e", p=128)
    w_a_v = w_a.rearrange("(c p) e -> p c e", p=128)
    w_b_v = w_b.rearrange("(c p) e -> p c e", p=128)
    w_o_v = w_o.rearrange("(c p) e -> p c e", p=128)

    with contextlib.ExitStack() as top:
        esems = {e: Sem(top.enter_context(nc.semaphore("es_" + e))) for e in ("tensor", "scalar", "vector", "gpsimd")}
        dpool = [Sem(top.enter_context(nc.semaphore("ds%d" % i))) for i in range(44)]
        P = Prog(nc, esems, dpool)

        uniq = [0]

        def sb(es, name, shape, dt):
            uniq[0] += 1
            return es.enter_context(nc.sbuf_tensor("%s_u%d" % (name, uniq[0]), list(shape), dt))

        def psbanks(es, n=8):
            uniq[0] += 1
            return [es.enter_context(nc.psum_tensor("pb%d_u%d" % (i, uniq[0]), [128, 512], F32)) for i in range(n)]

        def load_w(buf, dram_view, c0, ncol, nchunk, sem, waits):
            tok = None
            step = 8
            for q0 in range(0, nchunk, step):
                tok = P.dma("gpsimd", buf[:, q0:q0 + step, 0:ncol], dram_view[:, q0:q0 + step, c0:c0 + ncol], sem, waits)
            return tok

        def attn_factory(es, mode, PS, r_eng, mask_eng, yaT=None):
            cst = sb(es, "cst", [128, C_END], BF16)
            flag_t = sb(es, "flag_t", [128, 1], F32)
            ntri_h = sb(es, "ntri_h", [128, 128], BF16)
            none_h = sb(es, "none_h", [128, 128], BF16)
            if mode == "p":
                qTb = [sb(es, "qTb%d" % i, [128, NPR], BF16) for i in range(2)]
                kTb = [sb(es, "kTb%d" % i, [128, 2048], BF16) for i in range(2)]
                vbuf = [sb(es, "vbuf%d" % i, [128, 16, 128], BF16) for i in range(2)]
                szb = [sb(es, "szb%d" % i, [128, NPR], F32) for i in range(2)]
                yast = [sb(es, "yast%d" % i, [128, 512], BF16) for i in range(2)]
            else:
                ckb = [sb(es, "ckb%d" % i, [128, 8, PAST], BF16) for i in range(2)]
                cvb = [sb(es, "cvb%d" % i, [128, 8, 1024], BF16) for i in range(2)]
                qsm = [sb(es, "qsm%d" % i, [128, 8, SQ], BF16) for i in range(2)]
                ksm = [sb(es, "ksm%d" % i, [128, 8, SQ], BF16) for i in range(2)]
                vsm = [sb(es, "vsm%d" % i, [64, 1024], BF16) for i in range(2)]
                szs = [sb(es, "szs%d" % i, [128, 8, SQ], F32) for i in range(2)]
            e_t = [sb(es, "e_t%d" % i, [128, 512], F32) for i in range(2)]
            sp_t = [sb(es, "sp_t%d" % i, [128, 512], BF16) for i in range(3)]
            W_t = [sb(es, "W_t%d" % i, [128, 512], BF16) for i in range(2)]
            Rrow = [sb(es, "Rrow%d" % i, [32, 512], BF16) for i in range(2)]
            xc_t = [sb(es, "xc_t%d" % i, [128, 512], F32) for i in range(2)]
            ones_row = sb(es, "ones_row", [32, 128], BF16)
            one_t = sb(es, "one_t", [128, 1], F32)
            psA, psB, psO = PS
            nA, nB, nO = len(psA), len(psB), len(psO)
            s_c = P.dsem()
            s_q = [P.dsem(), P.dsem()]; s_k = [P.dsem(), P.dsem()]; s_v = [P.dsem(), P.dsem()]; s_z = [P.dsem(), P.dsem()]
            s_ck = [P.dsem(), P.dsem()]; s_cv = [P.dsem(), P.dsem()]; s_sm = [P.dsem(), P.dsem()]
            s_ya = [P.dsem(), P.dsem()]

            P.dma("sync", flag_t[:], flag_in[:], s_c)
            t_c = None
            for c0 in range(0, C_END, 1376):
                t_c = P.dma("gpsimd", cst[:, c0:c0 + 1376], consts_in[:, c0:c0 + 1376], s_c)
            t_f1 = P.op("vector", lambda e: e.tensor_scalar(out=ntri_h[:], in0=cst[:, C_TRI:C_TRI + 128], scalar1=flag_t[:, 0:1],
                                                             scalar2=None, op0=ALU.mult), [t_c])
            t_f2 = P.op("vector", lambda e: e.tensor_scalar(out=none_h[:], in0=cst[:, C_ONE:C_ONE + 128], scalar1=flag_t[:, 0:1],
                                                             scalar2=None, op0=ALU.mult), [t_c])
            P.op("vector", lambda e: e.memset(one_t[:], 1.0))
            P.op("vector", lambda e: e.memset(ones_row[:], 0.0))
            t_one = P.op("vector", lambda e: e.memset(ones_row[0:1, :], 1.0))
            ident_b = cst[:, C_ID:C_ID + 128]
            ntri = cst[:, C_TRI:C_TRI + 128]
            nones = cst[:, C_ONE:C_ONE + 128]

            jobs = []
            for h in range(NH if mode == "p" else 0):
                for T in range(2):
                    steps = []
                    for kb in range(4 * T + 3, -1, -1):
                        steps.append(dict(kind="p", kp=128, kcol=1024 + 128 * kb, vt=8 + kb,
                                          r=(kb - 4 * T) if kb >= 4 * T else None, flagged=False))
                    for bb in range(7, -1, -1):
                        steps.append(dict(kind="p", kp=128, kcol=128 * bb, vt=bb, r=None, flagged=True))
                    jobs.append(dict(kind="p", h=h, T=T, hb=h % 2, steps=steps))
            nsj = 0
            for s in range(NSQ if mode == "s" else 0):
                for g in range(2):
                    steps = [dict(kind="s", kp=64, own=True, r="s", flagged=False)]
                    for bb in range(7, -1, -1):
                        steps.append(dict(kind="s", kp=128, own=False, cb=bb, r=None, flagged=False))
                    jobs.append(dict(kind="s", s=s, g=g, sb=nsj % 2, steps=steps))
                    nsj += 1

            vh_v = vh_s.rearrange("(t p) e -> p t e", p=128)
            v_s_v = v_s.rearrange("(t p) e -> p t e", p=128)
            head_free = [[], []]
            samp_free = [[], []]
            head_tok = {}
            samp_tok = {}

            def load_head(h):
                hb = h % 2
                w = list(head_free[hb])
                tq = P.dma("sync", qTb[hb][:], qT_s[h][:, 0:NPR], s_q[hb], w)
                P.dma("sync", kTb[hb][:, 0:1024], kTh_s[h], s_k[hb], w)
                tk = P.dma("sync", kTb[hb][:, 1024:2048], kT_s[h][:, 0:NPR], s_k[hb], w)
                P.dma("sync", vbuf[hb][:, 0:8, :], vh_v[:, :, h * 128:(h + 1) * 128], s_v[hb], w)
                tv = P.dma("sync", vbuf[hb][:, 8:16, :], v_s_v[:, 0:8, h * 128:(h + 1) * 128], s_v[hb], w)
                tz = P.dma("sync", szb[hb][:], sza_s[h][:, 0:NPR], s_z[hb], w)
                head_tok[h] = dict(q=tq, k=tk, v=tv, z=tz)
                head_free[hb] = []

            def load_samp(ji):
                jb = jobs[ji]
                s, g, b2 = jb["s"], jb["g"], jb["sb"]
                w = list(samp_free[b2])
                tck = None
                for hh in range(0, 8, 2):
                    tck = P.dma("gpsimd", ckb[b2][:, hh:hh + 2, :],
                                ckT_in[s, g * 8 + hh:g * 8 + hh + 2].rearrange("h d k -> d h k"), s_ck[b2], w)
                tcv = None
                cvv = cv_in[s].rearrange("(t p) e -> p t e", p=128)
                for t0 in range(0, 8, 2):
                    tcv = P.dma("gpsimd", cvb[b2][:, t0:t0 + 2, :], cvv[:, t0:t0 + 2, g * 1024:(g + 1) * 1024], s_cv[b2], w)
                c0 = NPR + SQ * s
                P.dma("sync", qsm[b2][:], qT_s[g * 8:(g + 1) * 8, :, c0:c0 + SQ].rearrange("h d t -> d h t"), s_sm[b2], w)
                P.dma("sync", ksm[b2][:], kT_s[g * 8:(g + 1) * 8, :, c0:c0 + SQ].rearrange("h d t -> d h t"), s_sm[b2], w)
                P.dma("sync", vsm[b2][:], v_s[c0:c0 + SQ, g * 1024:(g + 1) * 1024], s_sm[b2], w)
                tsm = P.dma("sync", szs[b2][:], sza_s[g * 8:(g + 1) * 8, :, c0:c0 + SQ].rearrange("h d t -> d h t"), s_sm[b2], w)
                samp_tok[ji] = dict(ck=tck, cv=tcv, sm=tsm)
                samp_free[b2] = []

            flat = []
            for ji, jb in enumerate(jobs):
                n = len(jb["steps"])
                for si, st in enumerate(jb["steps"]):
                    st.update(job=ji, first=(si == 0), last=(si == n - 1))
                    flat.append(st)
            NS = len(flat)
            tok = [dict() for _ in range(NS)]
            psR_lastread = [None, None]
            psO_free = [None, None]
            yast_free = [None, None]
            ndiag = [0]
            last_evac = [None]

            def in_waits(st):
                jb = jobs[st["job"]]
                if jb["kind"] == "p":
                    ht = head_tok[jb["h"]]
                    return [ht["q"], ht["k"]], [ht["v"]], [ht["z"]]
                stt = samp_tok[st["job"]]
                return [stt["ck"], stt["sm"]], [stt["cv"], stt["sm"]], [stt["sm"]]

            def emit_S(s, st, ps, waits):
                jb = jobs[st["job"]]
                kp = st["kp"]
                if jb["kind"] == "p":
                    hb = jb["hb"]; T = jb["T"]
                    return [(lambda e, ps=ps, hb=hb, T=T, kc=st["kcol"]:
                             e.matmul(ps[:, :], lhsT=kTb[hb][:, kc:kc + 128], rhs=qTb[hb][:, 512 * T:512 * T + 512],
                                      start=True, stop=False, skip_group_check=True))]
                b2 = jb["sb"]
                fns = []
                for hh in range(8):
                    if st["own"]:
                        fns.append(lambda e, ps=ps, b2=b2, hh=hh:
                                   e.matmul(ps[0:64, hh * 64:(hh + 1) * 64], lhsT=ksm[b2][:, hh, :], rhs=qsm[b2][:, hh, :],
                                            start=(hh == 0), stop=False, skip_group_check=True))
                    else:
                        cb_ = st["cb"]
                        fns.append(lambda e, ps=ps, b2=b2, hh=hh, cb_=cb_:
                                   e.matmul(ps[:, hh * 64:(hh + 1) * 64], lhsT=ckb[b2][:, hh, cb_ * 128:(cb_ + 1) * 128],
                                            rhs=qsm[b2][:, hh, :], start=(hh == 0), stop=False, skip_group_check=True))
                return fns

            def stage1(s):
                st = flat[s]; kp = st["kp"]
                wq, wv, wz = in_waits(st)
                fns = emit_S(s, st, psA[s % nA], None)
                w = wq + [tok[s - nA].get("EA") if s >= nA else None]
                t = None
                for i, f in enumerate(fns):
                    t = P.op("tensor", f, w if i == 0 else (), signal=(i == len(fns) - 1))
                tok[s]["PE1"] = t
                tea = P.op("scalar", (lambda e, s=s, kp=kp: e.activation(out=e_t[s % 2][0:kp, :], in_=psA[s % nA][0:kp, :],
                                                                         func=AF.Exp, scale=1.0)),
                           [t, tok[s - 2].get("W") if s >= 2 else None])
                tok[s]["EA"] = tea
                if st["r"] is not None:
                    if st["r"] == "s":
                        m = cst[0:64, C_SM01:C_SM01 + 512]
                    else:
                        m = cst[:, C_M01 + 512 * st["r"]:C_M01 + 512 * st["r"] + 512]
                    tea = P.op(mask_eng, (lambda e, s=s, kp=kp, m=m: e.tensor_tensor(out=e_t[s % 2][0:kp, :], in0=e_t[s % 2][0:kp, :],
                                                                                      in1=m, op=ALU.mult)), [tea, t_c])
                tok[s]["EM"] = tea
                tok[s]["L"] = P.op("scalar", (lambda e, s=s, kp=kp: e.activation(out=sp_t[s % 3][0:kp, :], in_=e_t[s % 2][0:kp, :],
                                                                                  func=AF.Ln, bias=one_t[0:kp, 0:1], scale=1.0)),
                                   [tea, t_one, tok[s - 3].get("PE2") if s >= 3 else None])

            def stage2(s):
                st = flat[s]; kp = st["kp"]
                ps = psB[s % nB]
                tri = (ntri_h if st["flagged"] else ntri)
                has_r = not st["first"]
                fns = [lambda e, ps=ps, s=s, kp=kp, tri=tri, lastb=(not has_r):
                       e.matmul(ps[0:kp, :], lhsT=tri[0:kp, 0:kp], rhs=sp_t[s % 3][0:kp, :], start=True, stop=lastb,
                                skip_group_check=True)]
                if has_r:
                    fns.append(lambda e, ps=ps, s=s, kp=kp:
                               e.matmul(ps[0:kp, :], lhsT=ones_row[0:32, 0:kp], rhs=Rrow[(s - 1) % 2][0:32, :], start=False, stop=True,
                                        skip_group_check=True))
                w = [tok[s]["L"], t_f1, t_one]
                if s >= nB:
                    w += [tok[s - nB].get("EB"), tok[s - nB].get("R")]
                if has_r:
                    w.append(tok[s - 1].get("R"))
                t = None
                for i, f in enumerate(fns):
                    t = P.op("tensor", f, w if i == 0 else (), signal=(i == len(fns) - 1))
                tok[s]["PE2"] = t
                tok[s]["EB"] = P.op("scalar", (lambda e, s=s, kp=kp, ps=ps: e.activation(out=xc_t[s % 2][0:kp, :], in_=ps[0:kp, :],
                                                                                         func=AF.Exp, scale=1.0)),
                                    [t, tok[s - 2].get("W") if s >= 2 else None])
                if not st["last"]:
                    tok[s]["R"] = P.op("vector", (lambda e, s=s, ps=ps: e.tensor_copy(out=Rrow[s % 2][0:32, :], in_=ps[0:32, :])),
                                       [t, tok[s]["EB"]])
                tok[s]["W"] = P.op("vector", (lambda e, s=s, kp=kp: e.tensor_tensor(out=W_t[s % 2][0:kp, :], in0=e_t[s % 2][0:kp, :],
                                                                                    in1=xc_t[s % 2][0:kp, :], op=ALU.mult)),
                                   [tok[s]["EB"], tok[s]["EM"], tok[s - 2].get("PE3") if s >= 2 else None])

            def stage3(s):
                st = flat[s]; kp = st["kp"]
                ji = st["job"]; jb = jobs[ji]; jr = ji % nO
                wq, wv, wz = in_waits(st)
                fns = []
                if jb["kind"] == "p":
                    hb = jb["hb"]
                    fns.append(lambda e, s=s, hb=hb, vt=st["vt"], jr=jr, first=st["first"], last=st["last"]:
                               e.matmul(psO[jr][:, :], lhsT=vbuf[hb][:, vt, :], rhs=W_t[s % 2][:, :], start=first, stop=last, skip_group_check=True))
                else:
                    b2 = jb["sb"]
                    for hh in range(8):
                        if st["own"]:
                            fns.append(lambda e, s=s, b2=b2, hh=hh, jr=jr, first=st["first"], last=st["last"]:
                                       e.matmul(psO[jr][:, hh * 64:(hh + 1) * 64], lhsT=vsm[b2][0:64, hh * 128:(hh + 1) * 128],
                                                rhs=W_t[s % 2][0:64, hh * 64:(hh + 1) * 64], start=(first and hh == 0), stop=last, skip_group_check=True))
                        else:
                            fns.append(lambda e, s=s, b2=b2, hh=hh, jr=jr, cb_=st["cb"], first=st["first"], last=st["last"]:
                                       e.matmul(psO[jr][:, hh * 64:(hh + 1) * 64], lhsT=cvb[b2][:, cb_, hh * 128:(hh + 1) * 128],
                                                rhs=W_t[s % 2][:, hh * 64:(hh + 1) * 64], start=(first and hh == 0), stop=last, skip_group_check=True))
                w = wv + [tok[s]["W"], psO_free[jr] if st["first"] else None]
                t = None
                for i, f in enumerate(fns):
                    t = P.op("tensor", f, w if i == 0 else (), signal=(i == len(fns) - 1))
                tok[s]["PE3"] = t
                if st["last"]:
                    yb2 = ji % 2
                    if jb["kind"] == "p":
                        hb = jb["hb"]; T = jb["T"]; h = jb["h"]
                        te = P.op("vector", (lambda e, jr=jr, hb=hb, T=T, yb2=yb2:
                                             e.tensor_tensor(out=yast[yb2][:, :], in0=psO[jr][:, :],
                                                             in1=szb[hb][:, 512 * T:512 * T + 512], op=ALU.mult)),
                                  [t] + wz + [yast_free[yb2]])
                        yast_free[yb2] = P.dma("sync", ya_s[h][:, 512 * T:512 * T + 512], yast[yb2][:, :], s_ya[yb2], [te])
                        head_free[hb] = [t, te, tok[s]["PE2"]]
                    else:
                        b2 = jb["sb"]; s_ = jb["s"]; g = jb["g"]
                        c0 = NPR + SQ * s_
                        te = P.op("vector", (lambda e, jr=jr, b2=b2, g=g, c0=c0:
                                             e.tensor_tensor(out=yaT[:, g * 8:(g + 1) * 8, c0:c0 + SQ],
                                                             in0=psO[jr][:, :].rearrange("p (h t) -> p h t", t=SQ),
                                                             in1=szs[b2][:], op=ALU.mult)),
                                  [t] + wz)
                        samp_free[b2] = [t, te, tok[s]["PE2"]]
                        last_evac[0] = te
                    psO_free[jr] = te

            if mode == "p":
                load_head(0)
                load_head(1)
            else:
                load_samp(0)
                load_samp(1)
            tcks = [0]

            def tick():
                tck = tcks[0]
                if tck >= NS + 2:
                    return False
                tcks[0] += 1
                if tck < NS:
                    stage1(tck)
                if 0 <= tck - 1 < NS:
                    stage2(tck - 1)
                if 0 <= tck - 2 < NS:
                    s3 = tck - 2
                    stage3(s3)
                    st3 = flat[s3]
                    if st3["last"]:
                        jb = jobs[st3["job"]]
                        if jb["kind"] == "p" and jb["T"] == 1:
                            if jb["h"] + 2 < NH:
                                load_head(jb["h"] + 2)
                        elif jb["kind"] == "s":
                            nxt = st3["job"] + 2
                            if nxt < len(jobs):
                                load_samp(nxt)
                return True

            return tick, last_evac

        es_x = contextlib.ExitStack()
        xT = sb(es_x, "xT", [128, 32, NTX], BF16)
        s_xo = P.dsem()
        with contextlib.ExitStack() as es:
            xTh = sb(es, "xTh", [128, 32, NHIST], BF16)
            wbuf = [sb(es, "wbuf%d" % i, [128, 32, 256], BF16) for i in range(2)]
            kst = [sb(es, "kst%d" % i, [128, NHIST], BF16) for i in range(2)]
            vst = [sb(es, "vst%d" % i, [128, 8, 256], BF16) for i in range(2)]
            bias_t = sb(es, "bias_t", [128, 192], F32)
            bv_t = sb(es, "bv_t", [128, 2048], F32)
            bvf_t = sb(es, "bvf_t", [128, 2048], F32)
            flag_t = sb(es, "flag_t", [128, 1], F32)
            pb = psbanks(es, 6)
            s_xg = [P.dsem() for _ in range(8)]; s_c = P.dsem(); s_w = [P.dsem(), P.dsem()]
            s_ks = [P.dsem(), P.dsem()]; s_vs = [P.dsem(), P.dsem()]

            t_c = P.dma("sync", bias_t[:], bias_fm[:], s_c)
            t_c = P.dma("sync", bv_t[:], bv_bc[:], s_c)
            t_c = P.dma("sync", flag_t[:], flag_in[:], s_c)
            t_xg = [None] * 8
            t_x = None
            t_bvf = P.op("vector", lambda e: e.tensor_scalar(out=bvf_t[:], in0=bv_t[:], scalar1=flag_t[:, 0:1],
                                                              scalar2=None, op0=ALU.mult), [t_c])

            bank_free = {}
            wfree = [None, None]
            st_free_k = [None, None]
            st_free_v = [None, None]
            nw = 0
            item = 0
            for wt in range(8):
                b = nw % 2
                t_w = load_w(wbuf[b], w_in_v, 2048 + wt * 256, 256, 32, s_w[b], [wfree[b]])
                nw += 1
                if wt == 0:
                    for gq in range(8):
                        t_xg[gq] = P.dma("gpsimd", xTh[:, 4 * gq:4 * gq + 4, :], xT_hist[:, 4 * gq:4 * gq + 4, :], s_xg[gq])
                if wt == 2:
                    for q0 in range(0, 32, 4):
                        t_xo = P.dma("gpsimd", xT[:, q0:q0 + 4, :], xT_own[:, q0:q0 + 4, :], s_xo)
                last_mm = None
                for half in range(2):
                    h = 2 * wt + half
                    banks = [0, 1] if item % 2 == 0 else [3, 4]
                    waits = [t_w] + [bank_free.get(bk) for bk in banks]
                    for c in range(32):
                        for ti in range(2):
                            last = (c == 31 and ti == 1)
                            last_mm = P.op("tensor",
                                           (lambda e, bk=banks[ti], b=b, c=c, half=half, ti=ti:
                                            e.matmul(pb[bk][:, :], lhsT=wbuf[b][:, c, half * 128:(half + 1) * 128],
                                                     rhs=xTh[:, c, ti * 512:(ti + 1) * 512],
                                                     start=(c == 0), stop=(c == 31))),
                                           (waits if (c == 0 and ti == 0) else []) + ([t_xg[c // 4]] if (item == 0 and c % 4 == 0 and ti == 0) else []),
                                           signal=last)
                    kb = item % 2
                    tk = None
                    for ti in range(2):
                        tk = P.op("vector",
                                  (lambda e, bk=banks[ti], kb=kb, ti=ti, h=h:
                                   e.tensor_scalar(out=kst[kb][:, ti * 512:(ti + 1) * 512], in0=pb[bk][:, :],
                                                   scalar1=bias_t[:, 16 + h:17 + h], scalar2=None, op0=ALU.add)),
                                  [last_mm, st_free_k[kb], t_c])
                        bank_free[banks[ti]] = tk
                    st_free_k[kb] = P.dma("sync", kTh_s[h], kst[kb][:], s_ks[kb], [tk])
                    item += 1
                wfree[b] = last_mm
            vh_v = vh_s.rearrange("(t p) e -> p t e", p=128)
            nb = 0
            for cs in range(8):
                b = nw % 2
                t_w = load_w(wbuf[b], w_in_v, 4096 + cs * 256, 256, 32, s_w[b], [wfree[b]])
                nw += 1
                vb = cs % 2
                last_mm = None
                tv = None
                for tt in range(8):
                    bk = nb % 6
                    nb += 1
                    waits = [t_w, bank_free.get(bk)]
                    for c in range(32):
                        last_mm = P.op("tensor",
                                       (lambda e, bk=bk, b=b, c=c, tt=tt:
                                        e.matmul(pb[bk][:, 0:256], lhsT=xTh[:, c, tt * 128:(tt + 1) * 128],
                                                 rhs=wbuf[b][:, c, :], start=(c == 0), stop=(c == 31))),
                                       waits if c == 0 else (), signal=(c == 31))
                    tv = P.op("vector",
                              (lambda e, bk=bk, vb=vb, tt=tt, cs=cs:
                               e.scalar_tensor_tensor(out=vst[vb][:, tt, :], in0=pb[bk][:, 0:256], scalar=flag_t[:, 0:1],
                                                      in1=bvf_t[:, cs * 256:(cs + 1) * 256], op0=ALU.mult, op1=ALU.add)),
                              [last_mm, t_bvf, st_free_v[vb] if tt == 0 else None])
                    bank_free[bk] = tv
                st_free_v[vb] = P.dma("sync", vh_v[:, :, cs * 256:(cs + 1) * 256], vst[vb][:], s_vs[vb], [tv])
                wfree[b] = last_mm
            P.run_block()

        with contextlib.ExitStack() as es:
            wbuf = [sb(es, "wbuf%d" % i, [128, 32, 256], BF16) for i in range(2)]
            bias_t = sb(es, "bias_t", [128, 192], F32)
            bv_t = sb(es, "bv_t", [128, 2048], F32)
            flag_t = sb(es, "flag_t", [128, 1], F32)
            ident_f = sb(es, "ident_f", [128, 128], F32)
            fst = [sb(es, "fst%d" % i, [128, NTX], F32) for i in range(2)]
            bst = [sb(es, "bst%d" % i, [128, NT], BF16) for i in range(2)]
            ktst = [sb(es, "ktst%d" % i, [128, 10, 128], F32) for i in range(2)]
            vfst = [sb(es, "vfst%d" % i, [128, 10, 256], F32) for i in range(1)]
            vbst = [sb(es, "vbst%d" % i, [128, 10, 256], BF16) for i in range(1)]
            pb = psbanks(es, 8)
            s_x = P.dsem(); s_c = P.dsem(); s_w = [P.dsem(), P.dsem()]
            s_f = [P.dsem(), P.dsem()]; s_b = [P.dsem(), P.dsem()]; s_kt = [P.dsem(), P.dsem()]
            s_vf = [P.dsem(), P.dsem()]; s_vb = [P.dsem(), P.dsem()]; s_co = P.dsem()

            t_c = P.dma("sync", bias_t[:], bias_fm[:], s_c)
            P.dma("sync", bv_t[:], bv_bc[:], s_c)
            P.dma("sync", flag_t[:], flag_in[:], s_c)
            t_c = P.dma("sync", ident_f[:], consts_in[:, C_ID:C_ID + 128], s_c)
            t_x = t_xo

            bank_free = {}
            wfree = [None, None]
            state = {"nw": 0, "item": 0, "nb": 0}
            fst_free = [None, None]
            bst_free = [None, None]
            ktst_free = [None, None]
            pending_pe = []

            def next_w(c0):
                b = state["nw"] % 2
                state["nw"] += 1
                tw = load_w(wbuf[b], w_in_v, c0, 256, 32, s_w[b], [wfree[b]])
                return b, tw

            def mm_F(b, half, t_w):
                banks = [0, 1, 2] if state["item"] % 2 == 0 else [3, 4, 5]
                state["item"] += 1
                waits = [t_w, t_x] + [bank_free.get(bk) for bk in banks]
                last_mm = None
                for c in range(32):
                    for ti, (t0, n) in enumerate(TT3):
                        last = (c == 31 and ti == 2)
                        last_mm = P.op("tensor",
                                       (lambda e, bk=banks[ti], b=b, c=c, half=half, t0=t0, n=n:
                                        e.matmul(pb[bk][:, 0:n], lhsT=wbuf[b][:, c, half * 128:(half + 1) * 128],
                                                 rhs=xT[:, c, t0:t0 + n], start=(c == 0), stop=(c == 31))),
                                       waits if (c == 0 and ti == 0) else (), signal=last)
                wfree[b] = last_mm
                for f in pending_pe:
                    f()
                del pending_pe[:]
                return banks, last_mm

            def act_banks(banks, last_mm, dst, func, bcol, extra_waits, ncols3=TT3):
                tk = None
                for ti, (t0, n) in enumerate(ncols3):
                    tk = P.op("scalar",
                              (lambda e, bk=banks[ti], t0=t0, n=n:
                               e.activation(out=dst[:, t0:t0 + n], in_=pb[bk][:, 0:n], func=func,
                                            bias=bias_t[:, bcol:bcol + 1], scale=1.0)),
                              [last_mm, t_c] + list(extra_waits))
                    bank_free[banks[ti]] = tk
                return tk

            k_out_v = k_out.rearrange("(t p) e -> p t e", p=128)
            for wt in range(8):
                b, t_w = next_w(2048 + wt * 256)
                for half in range(2):
                    h = 2 * wt + half
                    banks, last_mm = mm_F(b, half, t_w)
                    i2 = h % 2
                    t_kf = act_banks(banks, last_mm, fst[i2], AF.Identity, 16 + h, [fst_free[i2]])
                    t_kb = P.op("vector", (lambda e, i2=i2: e.tensor_copy(out=bst[i2][:], in_=fst[i2][:, 0:NT])),
                                [t_kf, bst_free[i2]])
                    bst_free[i2] = P.dma("sync", kT_s[h], bst[i2][:], s_b[i2], [t_kb])

                    def k_transposes(i2=i2, h=h, t_kf=t_kf, t_kb=t_kb):
                        t_cp = None
                        t_tr = None
                        for g0 in range(0, 10, 4):
                            ng = min(4, 10 - g0)
                            bk = 6 + (g0 // 4) % 2
                            for j in range(ng):
                                tt = g0 + j
                                t_tr = P.op("tensor",
                                            (lambda e, bk=bk, j=j, tt=tt:
                                             e.transpose(out=pb[bk][:, j * 128:(j + 1) * 128],
                                                         in_=fst[i2][:, tt * 128:(tt + 1) * 128], identity=ident_f[:])),
                                            [t_kf, t_c, bank_free.get(bk)] if j == 0 else (), signal=(j == ng - 1))
                            t_cp = P.op("vector",
                                        (lambda e, bk=bk, g0=g0, ng=ng:
                                         e.tensor_copy(out=ktst[i2][:, g0:g0 + ng, :].rearrange("p a b -> p (a b)"),
                                                       in_=pb[bk][:, 0:ng * 128])),
                                        [t_tr, ktst_free[i2] if g0 == 0 else None])
                            bank_free[bk] = t_cp
                        ktst_free[i2] = P.dma("sync", k_out_v[:, :, h * 128:(h + 1) * 128], ktst[i2][:], s_kt[i2], [t_cp])
                        fst_free[i2] = t_cp
                    pending_pe.append(k_transposes)
                    fst_free[i2] = None

            for wt in range(8):
                b, t_w = next_w(0 + wt * 256)
                for half in range(2):
                    h = 2 * wt + half
                    banks, last_mm = mm_F(b, half, t_w)
                    i2 = h % 2
                    tq = None
                    for ti, (t0, n) in enumerate(TT3o):
                        tq = P.op("vector",
                                  (lambda e, bk=banks[ti], t0=t0, n=n, h=h, i2=i2:
                                   e.tensor_scalar(out=bst[i2][:, t0:t0 + n], in0=pb[bk][:, 0:n],
                                                   scalar1=bias_t[:, h:h + 1], scalar2=float(QSCALE),
                                                   op0=ALU.add, op1=ALU.mult)),
                                  [last_mm, t_c, bst_free[i2]])
                        bank_free[banks[ti]] = tq
                    bst_free[i2] = P.dma("sync", qT_s[h], bst[i2][:], s_b[i2], [tq])

            for wt in range(8):
                b, t_w = next_w(6144 + wt * 256)
                for half in range(2):
                    h = 2 * wt + half
                    banks, last_mm = mm_F(b, half, t_w)
                    i2 = h % 2
                    tz = act_banks(banks, last_mm, fst[i2], AF.Silu, 48 + h, [fst_free[i2]], TT3o)
                    fst_free[i2] = P.dma("sync", sza_s[h], fst[i2][:, 0:NT], s_f[i2], [tz])

            v_out_v = v_out.rearrange("(t p) e -> p t e", p=128)
            v_s_v = v_s.rearrange("(t p) e -> p t e", p=128)
            vf_free = [None, None]; vb_free = [None, None]
            for cs in range(8):
                b, t_w = next_w(4096 + cs * 256)
                vb = 0
                last_mm = None
                tv = tv2 = None
                for tt in range(10):
                    bk = state["nb"] % 6
                    state["nb"] += 1
                    waits = [t_w, t_x, bank_free.get(bk)]
                    for c in range(32):
                        last_mm = P.op("tensor",
                                       (lambda e, bk=bk, b=b, c=c, tt=tt:
                                        e.matmul(pb[bk][:, 0:256], lhsT=xT[:, c, tt * 128:(tt + 1) * 128],
                                                 rhs=wbuf[b][:, c, :], start=(c == 0), stop=(c == 31))),
                                       waits if c == 0 else (), signal=(c == 31))
                    if tt == 0:
                        for f in pending_pe:
                            f()
                        del pending_pe[:]
                    tv = P.op("vector", (lambda e, bk=bk, vb=vb, tt=tt, cs=cs:
                                         e.tensor_tensor(out=vfst[vb][:, tt, :], in0=pb[bk][:, 0:256],
                                                         in1=bv_t[:, cs * 256:(cs + 1) * 256], op=ALU.add)),
                              [last_mm, t_c, vf_free[vb] if tt == 0 else None])
                    bank_free[bk] = tv
                    tv2 = P.op("scalar", (lambda e, vb=vb, tt=tt: e.activation(out=vbst[vb][:, tt, :], in_=vfst[vb][:, tt, :],
                                                                                func=AF.Identity, scale=1.0)),
                               [tv, vb_free[vb] if tt == 0 else None])
                wfree[b] = last_mm
                vf_free[vb] = P.dma("sync", v_out_v[:, :, cs * 256:(cs + 1) * 256], vfst[vb][:], s_vf[vb], [tv, tv2])
                vb_free[vb] = P.dma("sync", v_s_v[:, :, cs * 256:(cs + 1) * 256], vbst[vb][:], s_vb[vb], [tv2])
            P.run_block()

        with contextlib.ExitStack() as es:
            wbuf = [sb(es, "wbuf%d" % i, [128, 32, 256], BF16) for i in range(2)]
            bias_t = sb(es, "bias_t", [128, 192], F32)
            flag_t = sb(es, "flag_t", [128, 1], F32)
            cw_t = sb(es, "cw_t", [128, 48], F32)
            cb_t = sb(es, "cb_t", [128, 16], F32)
            sc_t = sb(es, "sc_t", [128, 16, 4, 2], F32)
            fst = [sb(es, "fst%d" % i, [128, NT], F32) for i in range(2)]
            bst = [sb(es, "bst%d" % i, [128, NT], BF16) for i in range(2)]
            t1 = [sb(es, "t1_0", [128, NTX], F32)]
            ue = [sb(es, "ue0", [128, 1290], F32)]
            acc = [sb(es, "acc0", [128, NT], F32)]
            s1 = [sb(es, "s1_0", [128, NT], F32)]
            tmp2 = sb(es, "tmp2", [128, 2], F32)
            cout = sb(es, "cout", [128, 16, 10], F32)
            pb = psbanks(es, 8)
            s_c = P.dsem(); s_w = [P.dsem(), P.dsem()]
            s_f = [P.dsem(), P.dsem()]; s_b = [P.dsem(), P.dsem()]; s_co = P.dsem()

            t_c = P.dma("sync", bias_t[:], bias_fm[:], s_c)
            P.dma("sync", flag_t[:], flag_in[:], s_c)
            P.dma("sync", cw_t[:], cw_fm[:], s_c)
            P.dma("sync", cb_t[:], cb_fm[:], s_c)
            t_c = P.dma("sync", sc_t[:].rearrange("p a b c -> p (a b c)"), sc_fm[:], s_c)
            t_x = t_xo

            nbias_t = sb(es, "nbias_t", [128, 192], F32)
            one_m = sb(es, "one_m", [128, 1], F32)
            P.op("vector", lambda e: e.memset(one_m[:], 1.0))
            t_nb = P.op("vector", lambda e: e.tensor_scalar(out=nbias_t[:], in0=bias_t[:], scalar1=-1.0, scalar2=None, op0=ALU.mult), [t_c])
            tick, _ = attn_factory(es, "p", ([pb[3], pb[4]], [pb[5], pb[6]], [pb[7]]), "scalar", "vector")

            bank_free = {}
            wfree = [None, None]
            state = {"nw": 0, "mm": 0}
            fst_free = [None, None]
            bst_free = [None, None]

            def next_w(c0, ncol=256):
                b = state["nw"] % 2
                state["nw"] += 1
                tw = load_w(wbuf[b], w_in_v, c0, ncol, 32, s_w[b], [wfree[b]])
                return b, tw

            def mm_F(b, half, t_w, tiles=TT3):
                lm = []
                for ti, (t0, n) in enumerate(tiles):
                    waits = [t_w, t_x, bank_free.get(ti)]
                    tk = None
                    for c in range(32):
                        tk = P.op("tensor",
                                  (lambda e, ti=ti, b=b, c=c, half=half, t0=t0, n=n:
                                   e.matmul(pb[ti][:, 0:n], lhsT=wbuf[b][:, c, half * 128:(half + 1) * 128],
                                            rhs=xT[:, c, t0:t0 + n], start=(c == 0), stop=(c == 31))),
                                  waits if c == 0 else (), signal=(c == 31))
                        state["mm"] += n
                        if state["mm"] >= 11300:
                            state["mm"] -= 11300
                            tick()
                    lm.append(tk)
                wfree[b] = lm[-1]
                return lm

            def act_banks(lm, dst, func, bcol, extra_waits, tiles):
                tk = None
                for ti, (t0, n) in enumerate(tiles):
                    tk = P.op("scalar",
                              (lambda e, ti=ti, t0=t0, n=n:
                               e.activation(out=dst[:, t0:t0 + n], in_=pb[ti][:, 0:n], func=func,
                                            bias=bias_t[:, bcol:bcol + 1], scale=1.0)),
                              [lm[ti], t_c] + list(extra_waits))
                    bank_free[ti] = tk
                return tk

            sig_tile_tok = [None, None, None]

            def sig_banks(lm, dst, bcol, extra_waits, tiles):
                tk = None
                for ti, (t0, n) in enumerate(tiles):
                    t_a = P.op("scalar",
                               (lambda e, ti=ti, t0=t0, n=n:
                                e.activation(out=dst[:, t0:t0 + n], in_=pb[ti][:, 0:n], func=AF.Exp,
                                             bias=nbias_t[:, bcol:bcol + 1], scale=-1.0)),
                               [lm[ti], t_nb] + list(extra_waits))
                    t_b = P.op("scalar",
                               (lambda e, t0=t0, n=n:
                                e.activation(out=dst[:, t0:t0 + n], in_=dst[:, t0:t0 + n], func=AF.Ln,
                                             bias=one_m[:, 0:1], scale=1.0)), [t_a])
                    tk = P.op("scalar",
                              (lambda e, t0=t0, n=n:
                               e.activation(out=dst[:, t0:t0 + n], in_=dst[:, t0:t0 + n], func=AF.Exp, scale=-1.0)), [t_b])
                    bank_free[ti] = t_a
                    sig_tile_tok[ti] = tk
                return tk

            for wt in range(32):
                b, t_w = next_w(16384 + wt * 256)
                for half in range(2):
                    j = 2 * wt + half
                    lm = mm_F(b, half, t_w, TT3o)
                    i2 = j % 2
                    tz = sig_banks(lm, fst[i2], 128 + j, [fst_free[i2]], TT3o)
                    fst_free[i2] = P.dma("sync", sg_s[j], fst[i2][:], s_f[i2], [tz])

            conv_free = None
            t_cout = None
            p = 0
            for c in range(16):
                ue_s = ue[p][:, 1026:1290].rearrange("p (s t) -> p s t", t=66)
                acc_s = acc[p][:, 1024:1280].rearrange("p (s t) -> p s t", t=64)
                b, t_w = next_w(10240 + c * 128, 128)
                lm = mm_F(b, 0, t_w, TT3)
                t_t1 = act_banks(lm, t1[p], AF.Identity, 80 + c, [conv_free], TT3)
                b, t_w = next_w(12288 + c * 128, 128)
                lm = mm_F(b, 0, t_w, TT3)
                bcol = 96 + c
                ta = P.op("vector", (lambda e, bcol=bcol:
                                     e.scalar_tensor_tensor(out=ue[p][:, 2:514], in0=pb[0][:, 0:512],
                                                            scalar=bias_t[:, bcol:bcol + 1], in1=t1[p][:, 0:512],
                                                            op0=ALU.add, op1=ALU.mult)), [t_t1, lm[0], t_c, conv_free])
                bank_free[0] = ta
                tb = P.op("vector", (lambda e, bcol=bcol:
                                     e.scalar_tensor_tensor(out=ue[p][:, 514:1026], in0=pb[1][:, 0:512],
                                                            scalar=bias_t[:, bcol:bcol + 1], in1=t1[p][:, 512:1024],
                                                            op0=ALU.add, op1=ALU.mult)), [t_t1, lm[1]])
                bank_free[1] = tb
                tc_ = P.op("vector", (lambda e, bcol=bcol, ue_s=ue_s:
                                      e.scalar_tensor_tensor(out=ue_s[:, :, 2:66],
                                                             in0=pb[2][:, 0:256].rearrange("p (s t) -> p s t", t=64),
                                                             scalar=bias_t[:, bcol:bcol + 1],
                                                             in1=t1[p][:, 1024:1280].rearrange("p (s t) -> p s t", t=64),
                                                             op0=ALU.add, op1=ALU.mult)), [t_t1, lm[2]])
                td = P.op("vector", (lambda e, bcol=bcol:
                                     e.scalar_tensor_tensor(out=tmp2[:, :], in0=pb[2][:, 256:258],
                                                            scalar=bias_t[:, bcol:bcol + 1], in1=t1[p][:, 1280:1282],
                                                            op0=ALU.add, op1=ALU.mult)), [t_t1, lm[2]])
                bank_free[2] = td
                te = P.op("vector", (lambda e: e.tensor_scalar(out=ue[p][:, 0:2], in0=tmp2[:, :],
                                                                scalar1=flag_t[:, 0:1], scalar2=None, op0=ALU.mult)), [td])
                tf = P.op("vector", (lambda e, c=c, ue_s=ue_s: e.tensor_copy(out=ue_s[:, :, 0:2], in_=sc_t[:, c, :, :])), [t_c])
                w0 = cw_t[:, 3 * c:3 * c + 1]; w1 = cw_t[:, 3 * c + 1:3 * c + 2]; w2 = cw_t[:, 3 * c + 2:3 * c + 3]
                cbv = cb_t[:, c:c + 1]
                t3 = P.op("vector", (lambda e, w0=w0, cbv=cbv:
                                     e.tensor_scalar(out=acc[p][:, 0:1024], in0=ue[p][:, 0:1024], scalar1=w0, scalar2=cbv,
                                                     op0=ALU.mult, op1=ALU.add)), [ta, tb, te])
                t3 = P.op("vector", (lambda e, w1=w1:
                                     e.scalar_tensor_tensor(out=acc[p][:, 0:1024], in0=ue[p][:, 1:1025], scalar=w1,
                                                            in1=acc[p][:, 0:1024], op0=ALU.mult, op1=ALU.add)), [t3])
                t3 = P.op("vector", (lambda e, w2=w2:
                                     e.scalar_tensor_tensor(out=acc[p][:, 0:1024], in0=ue[p][:, 2:1026], scalar=w2,
                                                            in1=acc[p][:, 0:1024], op0=ALU.mult, op1=ALU.add)), [t3])
                t4 = P.op("vector", (lambda e, ue_s=ue_s, acc_s=acc_s, w0=w0, cbv=cbv:
                                     e.tensor_scalar(out=acc_s, in0=ue_s[:, :, 0:64], scalar1=w0, scalar2=cbv,
                                                     op0=ALU.mult, op1=ALU.add)), [tc_, tf])
                t4 = P.op("vector", (lambda e, ue_s=ue_s, acc_s=acc_s, w1=w1:
                                     e.scalar_tensor_tensor(out=acc_s, in0=ue_s[:, :, 1:65], scalar=w1, in1=acc_s,
                                                            op0=ALU.mult, op1=ALU.add)), [t4])
                t4 = P.op("vector", (lambda e, ue_s=ue_s, acc_s=acc_s, w2=w2:
                                     e.scalar_tensor_tensor(out=acc_s, in0=ue_s[:, :, 2:66], scalar=w2, in1=acc_s,
                                                            op0=ALU.mult, op1=ALU.add)), [t4])
                P.op("vector", (lambda e, c=c: e.tensor_copy(out=cout[:, c, 0:2], in_=ue[p][:, 1024:1026])), [ta, tb])
                t_cout = P.op("vector", (lambda e, c=c, ue_s=ue_s:
                                         e.tensor_copy(out=cout[:, c, 2:10].rearrange("p (s t) -> p s t", t=2),
                                                       in_=ue_s[:, :, 64:66])), [tc_])
                t_acc = t_cout
                b, t_w = next_w(14336 + c * 128, 128)
                lm = mm_F(b, 0, t_w, TT3o)
                t_sg = sig_banks(lm, s1[p], 112 + c, [conv_free], TT3o)
                t_s1 = None
                for ti, (t0, n) in enumerate(TT3o):
                    t_s1 = P.op("vector", (lambda e, ti=ti, t0=t0, n=n, bc=112 + c:
                                           e.scalar_tensor_tensor(out=s1[p][:, t0:t0 + n], in0=pb[ti][:, 0:n],
                                                                  scalar=bias_t[:, bc:bc + 1], in1=s1[p][:, t0:t0 + n],
                                                                  op0=ALU.add, op1=ALU.mult)), [sig_tile_tok[ti], t_c])
                    bank_free[ti] = t_s1
                b, t_w = next_w(8192 + c * 128, 128)
                lm = mm_F(b, 0, t_w, TT3o)
                bcol = 64 + c
                ty = None
                for ti, (t0, n) in enumerate(TT3o):
                    ty = P.op("vector", (lambda e, ti=ti, t0=t0, n=n, bcol=bcol:
                                         e.scalar_tensor_tensor(out=s1[p][:, t0:t0 + n], in0=pb[ti][:, 0:n],
                                                                scalar=bias_t[:, bcol:bcol + 1], in1=s1[p][:, t0:t0 + n],
                                                                op0=ALU.add, op1=ALU.mult)), [lm[ti], t_s1, t_c])
                    bank_free[ti] = ty
                i2 = c % 2
                ty = P.op("vector", (lambda e, i2=i2: e.tensor_tensor(out=bst[i2][:], in0=s1[p][:], in1=acc[p][:], op=ALU.mult)),
                          [ty, t_acc, bst_free[i2]])
                conv_free = ty
                bst_free[i2] = P.dma("sync", yb_s[c], bst[i2][:], s_b[i2], [ty])
            P.dma("sync", conv_out[:], cout[:].rearrange("p a b -> p (a b)"), s_co, [t_cout])
            while tick():
                pass
            P.run_block()
        es_x.close()

        es_wo = contextlib.ExitStack()
        wob0 = sb(es_wo, "wob0", [128, 32, 512], BF16)
        es_y = contextlib.ExitStack()
        yaT = sb(es_y, "yaT", [128, 16, NT], BF16)
        s_yb = P.dsem()
        with contextlib.ExitStack() as es:
            pb = psbanks(es, 8)
            tick, last_evac = attn_factory(es, "s", ([pb[0], pb[1]], [pb[2], pb[3]], [pb[6], pb[7]]),
                                           "vector", "gpsimd", yaT=yaT)
            for hh in range(16):
                P.dma("sync", yaT[:, hh, 0:NPR], ya_s[hh][:, 0:NPR], s_yb)
            while tick():
                pass
            if debug:
                s_dbg = P.dsem()
                for hh in range(16):
                    P.dma("sync", ya_s[hh][:, NPR:NT], yaT[:, hh, NPR:NT], s_dbg, [last_evac[0]])
            P.run_block()


        es_y2 = contextlib.ExitStack()
        ybT = sb(es_y2, "ybT", [128, 16, NT], BF16)
        with contextlib.ExitStack() as es:
            wab = [sb(es, "wab%d" % i, [128, 16, 256], BF16) for i in range(4)]
            sgb = [sb(es, "sgb%d" % i, [128, NT], F32) for i in range(4)]
            ta_t = [sb(es, "ta_t%d" % i, [128, NT], F32) for i in range(2)]
            tb_t = sb(es, "tb_t", [128, NT], F32)
            mst = [sb(es, "mst%d" % i, [128, NT], BF16) for i in range(2)]
            pb = psbanks(es, 6)
            s_y = P.dsem(); s_w = [P.dsem() for _ in range(4)]; s_g = [P.dsem() for _ in range(4)]
            s_m = [P.dsem(), P.dsem()]
            t_y = None
            bank_free = {}
            wfree = [None] * 4
            sg_free = [None] * 4
            mst_free = [None, None]
            item = 0
            wtok = {}
            for jp in range(16):
                for br in range(2):
                    wi = (2 * jp + br) % 4
                    wtok[(jp, br)] = load_w(wab[wi], (w_a_v if br == 0 else w_b_v), jp * 256, 256, 16, s_w[wi], [wfree[wi]])
                for half in range(2):
                    j = 2 * jp + half
                    t_a = None
                    for br in range(2):
                        wi = (2 * jp + br) % 4
                        gi = (2 * j + br) % 4
                        t_g = P.dma("sync", sgb[gi][:], sg_s[j + 32 * br], s_g[gi], [sg_free[gi]])
                        if jp == 0 and half == 0 and br == 0:
                            for hh in range(16):
                                t_y = P.dma("sync", ybT[:, hh, :], yb_s[hh], s_y)
                        banks = [0, 1, 2] if item % 2 == 0 else [3, 4, 5]
                        item += 1
                        src = yaT if br == 0 else ybT
                        waits = [wtok[(jp, br)], (t_y if br == 1 else None)] + [bank_free.get(bk) for bk in banks]
                        last_mm = None
                        for c in range(16):
                            for ti, (t0, n) in enumerate(TT3o):
                                last = (c == 15 and ti == 2)
                                last_mm = P.op("tensor",
                                               (lambda e, bk=banks[ti], wi=wi, c=c, half=half, t0=t0, n=n, src=src:
                                                e.matmul(pb[bk][:, 0:n], lhsT=wab[wi][:, c, half * 128:(half + 1) * 128],
                                                         rhs=src[:, c, t0:t0 + n], start=(c == 0), stop=(c == 15))),
                                               waits if (c == 0 and ti == 0) else (), signal=last)
                        if half == 1:
                            wfree[wi] = last_mm
                        j2 = j % 2
                        dst = ta_t[j2] if br == 0 else tb_t
                        tk = None
                        for ti, (t0, n) in enumerate(TT3o):
                            tk = P.op("vector", (lambda e, bk=banks[ti], t0=t0, n=n, gi=gi, dst=dst:
                                                 e.tensor_tensor(out=dst[:, t0:t0 + n], in0=pb[bk][:, 0:n], in1=sgb[gi][:, t0:t0 + n],
                                                                 op=ALU.mult)), [last_mm, t_g, t_a if br == 1 else mst_free[j2]])
                            bank_free[banks[ti]] = tk
                        sg_free[gi] = tk
                        if br == 0:
                            t_a = tk
                        else:
                            tm = P.op("vector", (lambda e, j2=j2: e.tensor_tensor(out=mst[j2][:], in0=ta_t[j2][:], in1=tb_t[:], op=ALU.add)),
                                      [tk, t_a, mst_free[j2]])
                            mst_free[j2] = P.dma("sync", mg_s[j], mst[j2][:], s_m[j2], [tm])
                            tb_guard = tm
            s_wo0 = P.dsem()
            load_w(wob0, w_o_v, 0, 512, 32, s_wo0, [])
            P.run_block()
        es_y2.close()
        es_y.close()

        with contextlib.ExitStack() as es:
            mT = sb(es, "mT", [128, 32, NT], BF16)
            wob = [wob0, sb(es, "wob1", [128, 32, 512], BF16)]
            xt_t = [sb(es, "xt_t%d" % i, [128, 512], F32) for i in range(3)]
            hst = [sb(es, "hst%d" % i, [128, 512], F32) for i in range(3)]
            pb = psbanks(es, 6)
            s_m = P.dsem(); s_w = [P.dsem(), P.dsem()]; s_x = [P.dsem() for _ in range(3)]; s_h = [P.dsem() for _ in range(3)]
            t_m = None
            for cc in range(32):
                t_m = P.dma("sync", mT[:, cc, :], mg_s[cc], s_m)
            bank_free = {}
            wfree = [None, None]
            x_free = [None] * 3
            h_free = [None] * 3
            it = 0
            for n8 in range(8):
                b = n8 % 2
                if n8 == 0:
                    t_w = None
                else:
                    t_w = load_w(wob[b], w_o_v, n8 * 512, 512, 32, s_w[b], [wfree[b]])
                last_mm = None
                for tt in range(10):
                    bk = it % 6
                    xi = it % 3
                    it += 1
                    t_xl = P.dma("sync", xt_t[xi][:], x_tok[tt * 128:(tt + 1) * 128, n8 * 512:(n8 + 1) * 512], s_x[xi], [x_free[xi]])
                    waits = [t_w, t_m, bank_free.get(bk)]
                    for c in range(32):
                        last_mm = P.op("tensor",
                                       (lambda e, bk=bk, b=b, c=c, tt=tt:
                                        e.matmul(pb[bk][:, :], lhsT=mT[:, c, tt * 128:(tt + 1) * 128], rhs=wob[b][:, c, :],
                                                 start=(c == 0), stop=(c == 31))),
                                       waits if c == 0 else (), signal=(c == 31))
                    th = P.op("vector", (lambda e, bk=bk, xi=xi:
                                         e.scalar_tensor_tensor(out=hst[xi][:], in0=xt_t[xi][:], scalar=float(DN_ALPHA),
                                                                in1=pb[bk][:, :], op0=ALU.mult, op1=ALU.add)),
                              [last_mm, t_xl, h_free[xi]])
                    bank_free[bk] = th
                    x_free[xi] = th
                    h_free[xi] = P.dma("sync", h_s[tt * 128:(tt + 1) * 128, n8 * 512:(n8 + 1) * 512], hst[xi][:], s_h[xi], [th])
                wfree[b] = last_mm
            P.run_block()
        es_wo.close()

        with contextlib.ExitStack() as es:
            g_t = sb(es, "g_t", [128, D], F32)
            b_t = sb(es, "b_t", [128, D], F32)
            hb_ = [sb(es, "hb%d" % i, [128, D], F32) for i in range(3)]
            yb_ = [sb(es, "yb%d" % i, [128, D], F32) for i in range(3)]
            stats = [sb(es, "stats%d" % i, [128, 8, 6], F32) for i in range(3)]
            mv = [sb(es, "mv%d" % i, [128, 2], F32) for i in range(3)]
            rstd = [sb(es, "rstd%d" % i, [128, 1], F32) for i in range(3)]
            nmr = [sb(es, "nmr%d" % i, [128, 1], F32) for i in range(3)]
            eps_t = sb(es, "eps_t", [128, 1], F32)
            s_c = P.dsem(); s_hl = [P.dsem() for _ in range(3)]; s_yo = [P.dsem() for _ in range(3)]
            hfree = [None] * 3
            yfree = [None] * 3
            t_ls = {}
            t_ls[0] = P.dma("sync", hb_[0][:], h_s[0:128, :], s_hl[0])
            P.dma("sync", g_t[:], lng_bc[:], s_c)
            t_c = P.dma("sync", b_t[:], lnb_bc[:], s_c)
            t_ls[1] = P.dma("sync", hb_[1][:], h_s[128:256, :], s_hl[1])
            t_eps = P.op("vector", lambda e: e.memset(eps_t[:], float(LN_EPS)))
            small_free = [None] * 3
            for tt in range(10):
                i2 = tt % 3
                if tt + 2 < 10:
                    j2 = (tt + 2) % 3
                    t_ls[tt + 2] = P.dma("sync", hb_[j2][:], h_s[(tt + 2) * 128:(tt + 3) * 128, :], s_hl[j2], [hfree[j2]])
                t_l = t_ls[tt]
                t_s = None
                for k8 in range(8):
                    t_s = P.op("vector", (lambda e, i2=i2, k8=k8: e.bn_stats(out=stats[i2][:, k8, :], in_=hb_[i2][:, k8 * 512:(k8 + 1) * 512])),
                               [t_l, small_free[i2]])
                t_a = P.op("vector", (lambda e, i2=i2: e.bn_aggr(out=mv[i2][:], in_=stats[i2][:].rearrange("p a b -> p (a b)"))), [t_s])
                t_q = P.op("scalar", (lambda e, i2=i2: e.activation(out=rstd[i2][:], in_=mv[i2][:, 1:2], func=AF.Sqrt, bias=eps_t[:, 0:1], scale=1.0)),
                           [t_a, t_eps])
                t_r = P.op("vector", (lambda e, i2=i2: e.reciprocal(out=rstd[i2][:], in_=rstd[i2][:])), [t_q])
                t_m = P.op("vector", (lambda e, i2=i2: e.tensor_scalar(out=nmr[i2][:], in0=mv[i2][:, 0:1], scalar1=rstd[i2][:, 0:1], scalar2=-1.0,
                                                                        op0=ALU.mult, op1=ALU.mult)), [t_r])
                t_n = P.op("scalar", (lambda e, i2=i2: e.activation(out=yb_[i2][:], in_=hb_[i2][:], func=AF.Identity,
                                                                     bias=nmr[i2][:, 0:1], scale=rstd[i2][:, 0:1])), [t_m, yfree[i2]])
                hfree[i2] = t_n
                small_free[i2] = t_n
                t_g = P.op("vector", (lambda e, i2=i2: e.tensor_tensor(out=yb_[i2][:], in0=yb_[i2][:], in1=g_t[:], op=ALU.mult)), [t_n, t_c])
                t_b = P.op("gpsimd", (lambda e, i2=i2: e.tensor_tensor(out=yb_[i2][:], in0=yb_[i2][:], in1=b_t[:], op=ALU.add)), [t_g])
                yfree[i2] = P.dma("sync", y_out[tt * 128:(tt + 1) * 128, :], yb_[i2][:], s_yo[i2], [t_b])
            P.run_block()
    return nc


def _consts():
    c = np.zeros((128, C_END), np.float32)
    c[:, C_ID:C_ID + 128] = np.eye(128, dtype=np.float32)
    kp = np.arange(128)[:, None]
    k = np.arange(128)[None, :]
    c[:, C_TRI:C_TRI + 128] = np.where(kp >= k, -1.0, 0.0)
    c[:, C_ONE:C_ONE + 128] = -1.0
    q = np.arange(512)[None, :]
    for r in range(4):
        valid = (128 * r + kp) < q
        c[:, C_M01 + 512 * r:C_M01 + 512 * (r + 1)] = valid.astype(np.float32)
        c[:, C_MB + 512 * r:C_MB + 512 * (r + 1)] = np.where(valid, 0.0, -30000.0)
    qq = (np.arange(512) % 64)[None, :]
    valid = kp < qq
    c[:, C_SM01:C_SM01 + 512] = valid.astype(np.float32)
    c[:, C_SMB:C_SMB + 512] = np.where(valid, 0.0, -30000.0)
    return c


def _fm(v):
    return np.ascontiguousarray(v.reshape(-1, 128).T)


def make_in_maps(x_prompt, x_sample, cache_k, cache_v, state_conv, w_in, b_in, conv_w, conv_b,
                 w_a, w_b, w_o, ln_g, ln_b, cores=range(8)):
    f = np.float32
    w_in0 = np.ascontiguousarray(w_in[0], dtype=f)
    w_a0 = np.ascontiguousarray(w_a[0], dtype=f)
    w_b0 = np.ascontiguousarray(w_b[0], dtype=f)
    w_o0 = np.ascontiguousarray(w_o[0], dtype=f)
    bias_fm = _fm(np.asarray(b_in[0], f))
    bv_bc = np.ascontiguousarray(np.broadcast_to(np.asarray(b_in[0, 4096:6144], f)[None, :], (128, 2048)))
    cw = np.asarray(conv_w[0], f)
    cw_fm = np.ascontiguousarray(cw.reshape(3, 16, 128).transpose(2, 1, 0).reshape(128, 48))
    cb_fm = _fm(np.asarray(conv_b[0], f))
    lng = np.ascontiguousarray(np.broadcast_to(np.asarray(ln_g[0], f)[None, :], (128, D)))
    lnb = np.ascontiguousarray(np.broadcast_to(np.asarray(ln_b[0], f)[None, :], (128, D)))
    consts = _consts()
    maps = []
    for c in cores:
        bq, half = c // 2, c % 2
        xp = np.asarray(x_prompt[bq], f)
        own = xp[half * 1024:(half + 1) * 1024]
        xs = np.asarray(x_sample[4 * c:4 * c + 4], f).reshape(256, D)
        if half == 1:
            hist = xp[0:1024]
            flag = np.ones((128, 1), f)
        else:
            hist = np.zeros((1024, D), f)
            flag = np.zeros((128, 1), f)
        x_tok = np.ascontiguousarray(np.concatenate([own, xs], 0))
        x_ext = np.concatenate([own, xs, hist[1022:1024]], 0)
        xT_own = np.ascontiguousarray(x_ext.T.reshape(32, 128, NTX).transpose(1, 0, 2))
        xT_hist = np.ascontiguousarray(hist.T.reshape(32, 128, NHIST).transpose(1, 0, 2))
        ck = np.asarray(cache_k[0, 4 * c:4 * c + 4], f)
        ckT = np.ascontiguousarray(ck.transpose(0, 2, 3, 1))
        cv = np.ascontiguousarray(np.asarray(cache_v[0, 4 * c:4 * c + 4], f).reshape(4, PAST, 2048))
        sc = np.asarray(state_conv[0, 4 * c:4 * c + 4], f)
        sc_fm = np.ascontiguousarray(sc.reshape(4, 2, 16, 128).transpose(3, 2, 0, 1).reshape(128, 128))
        maps.append(dict(xT_own=xT_own, xT_hist=xT_hist, x_tok=x_tok, w_in=w_in0, w_a=w_a0, w_b=w_b0, w_o=w_o0,
                         bias_fm=bias_fm, bv_bc=bv_bc, cw_fm=cw_fm, cb_fm=cb_fm, sc_fm=sc_fm, lng_bc=lng, lnb_bc=lnb,
                         flag=flag, consts=consts, ckT=ckT, cv=cv))
    return maps


_NC_CACHE = {}


def kernel(x_prompt, x_sample, cache_k, cache_v, state_conv, w_in, b_in, conv_w, conv_b,
           w_a, w_b, w_o, ln_g, ln_b):
    args = [np.asarray(a) for a in (x_prompt, x_sample, cache_k, cache_v, state_conv, w_in, b_in, conv_w, conv_b,
                                    w_a, w_b, w_o, ln_g, ln_b)]
    maps = make_in_maps(*args)
    if "nc" not in _NC_CACHE:
        _NC_CACHE["nc"] = build_program(False)
    nc = _NC_CACHE["nc"]
    res = run_bass_kernel_spmd(nc, maps, core_ids=list(range(8)))
    R = res.results
    f = np.float32
    y_prompt = np.zeros((4, 2048, D), f)
    y_sample = np.zeros((32, 64, D), f)
    k_prompt = np.zeros((1, 4, 2048, NH, HD), f)
    v_prompt = np.zeros((1, 4, 2048, NH, HD), f)
    conv_prompt = np.zeros((1, 4, 2, 2048), f)
    k_new = np.zeros((1, 32, 64, NH, HD), f)
    v_new = np.zeros((1, 32, 64, NH, HD), f)
    conv_sample = np.zeros((1, 32, 2, 2048), f)
    for c in range(8):
        bq, half = c // 2, c % 2
        r = R[c]
        y = np.asarray(r["y_out"]); k = np.asarray(r["k_out"]); v = np.asarray(r["v_out"])
        co = np.asarray(r["conv_out"]).reshape(128, 16, 10)
        sl = slice(half * 1024, (half + 1) * 1024)
        y_prompt[bq, sl] = y[0:1024]
        y_sample[4 * c:4 * c + 4] = y[1024:].reshape(4, 64, D)
        k_prompt[0, bq, sl] = k[0:1024].reshape(1024, NH, HD)
        v_prompt[0, bq, sl] = v[0:1024].reshape(1024, NH, HD)
        k_new[0, 4 * c:4 * c + 4] = k[1024:].reshape(4, 64, NH, HD)
        v_new[0, 4 * c:4 * c + 4] = v[1024:].reshape(4, 64, NH, HD)
        ch = co.transpose(2, 1, 0).reshape(10, 2048)
        if half == 1:
            conv_prompt[0, bq] = ch[0:2]
        conv_sample[0, 4 * c:4 * c + 4] = ch[2:10].reshape(4, 2, 2048)
    return (y_prompt, y_sample, k_prompt, v_prompt, conv_prompt, k_new, v_new, conv_sample)
```

```python
import contextlib
import numpy as np
import concourse.bass as bass
import concourse.mybir as mybir
from concourse.bass_utils import run_bass_kernel_spmd

F32 = mybir.dt.float32
BF16 = mybir.dt.bfloat16
AF = mybir.ActivationFunctionType
ALU = mybir.AluOpType

D = 4096
NH = 16
HD = 128
NIN = 24576
NT = 1280
NTX = 1282
NHIST = 1024
NPR = 1024
NSQ = 4
SQ = 64
PAST = 1024
DN_ALPHA = 2.0 ** 0.25
LN_EPS = 1e-5
QSCALE = HD ** -0.5
TT3 = [(0, 512), (512, 512), (1024, 258)]
TT3o = [(0, 512), (512, 512), (1024, 256)]

C_ID, C_TRI, C_ONE, C_M01, C_MB, C_SM01, C_SMB, C_END = 0, 128, 256, 384, 2432, 4480, 4992, 5504


class Sem:
    def __init__(self, h):
        self.h = h
        self.cnt = 0


class Prog:
    ENGS = ("tensor", "scalar", "vector", "gpsimd", "sync")

    def __init__(self, nc, esems, dpool):
        self.nc = nc
        self.es = esems
        self.dpool = dpool
        self.used = []
        self.q = {e: [] for e in self.ENGS}
        self.dtoks = {}

    def dsem(self):
        s = self.dpool.pop()
        self.used.append(s)
        return s

    def op(self, eng, fn, waits=(), signal=True):
        tok = None
        if signal:
            s = self.es[eng]
            s.cnt += 1
            tok = (s, s.cnt)
        self.q[eng].append(([w for w in waits if w is not None], fn, tok, 1))
        return tok

    def dma(self, eng, out, in_, sem, waits=()):
        sem.cnt += 16
        tok = (sem, sem.cnt)
        self.dtoks[id(sem)] = tok
        self.q[eng].append(([w for w in waits if w is not None],
                            (lambda e, o=out, i=in_: e.dma_start(out=o, in_=i)), tok, 16))
        return tok

    def check_deadlock(self):
        cnt = getattr(self, "_simcnt", {})
        ptr = {e: 0 for e in self.ENGS}
        progress = True
        while progress:
            progress = False
            for e in self.ENGS:
                q = self.q[e]
                while ptr[e] < len(q):
                    waits, fn, tok, inc = q[ptr[e]]
                    if all(cnt.get(id(s_), 0) >= v for (s_, v) in waits):
                        if tok is not None:
                            cnt[id(tok[0])] = cnt.get(id(tok[0]), 0) + inc
                        ptr[e] += 1
                        progress = True
                    else:
                        break
        stuck = {e: ptr[e] for e in self.ENGS if ptr[e] < len(self.q[e])}
        self._simcnt = cnt
        if stuck:
            msg = []
            for e, p in stuck.items():
                waits, fn, tok, inc = self.q[e][p]
                msg.append("%s@%d/%d waits %s" % (e, p, len(self.q[e]),
                           [(next((k for k, v_ in self.es.items() if v_ is s_), "dsem"), v, cnt.get(id(s_), 0)) for (s_, v) in waits]))
            raise RuntimeError("semaphore plan deadlocks: " + "; ".join(msg))

    def run_block(self):
        self.q["sync"].append((list(self.dtoks.values()), None, None, 0))
        self.check_deadlock()
        nc = self.nc
        qs = self.q

        def mk(name):
            def body(e):
                waited = {}
                for waits, fn, tok, inc in qs[name]:
                    for (s, v) in waits:
                        if waited.get(id(s), 0) < v:
                            e.wait_ge(s.h, v)
                            waited[id(s)] = v
                    if fn is None:
                        continue
                    ins = fn(e)
                    if tok is not None:
                        ins.then_inc(tok[0].h, inc)
            return body

        with nc.Block() as block:
            block.tensor(mk("tensor"))
            block.scalar(mk("scalar"))
            block.vector(mk("vector"))
            block.gpsimd(mk("gpsimd"))
            block.sync(mk("sync"))
        self.q = {e: [] for e in self.ENGS}
        self.dtoks = {}
        self.dpool.extend(self.used)
        self.used = []


def build_program(debug=False):
    nc = bass.Bass("TRN2", target_bir_lowering=False)

    def din(name, shape, dt=F32):
        return nc.dram_tensor(name, list(shape), dt, kind="ExternalInput").ap()

    def dout(name, shape, dt=F32):
        return nc.dram_tensor(name, list(shape), dt, kind="ExternalOutput").ap()

    def dscr(name, shape, dt):
        return nc.dram_tensor(name, list(shape), dt, kind="ExternalOutput" if debug else "Internal").ap()

    xT_own = din("xT_own", [128, 32, NTX])
    xT_hist = din("xT_hist", [128, 32, NHIST])
    x_tok = din("x_tok", [NT, D])
    w_in = din("w_in", [D, NIN])
    w_a = din("w_a", [2048, D])
    w_b = din("w_b", [2048, D])
    w_o = din("w_o", [D, D])
    bias_fm = din("bias_fm", [128, 192])
    bv_bc = din("bv_bc", [128, 2048])
    cw_fm = din("cw_fm", [128, 16 * 3])
    cb_fm = din("cb_fm", [128, 16])
    sc_fm = din("sc_fm", [128, 16 * 8])
    lng_bc = din("lng_bc", [128, D])
    lnb_bc = din("lnb_bc", [128, D])
    flag_in = din("flag", [128, 1])
    consts_in = din("consts", [128, C_END])
    ckT_in = din("ckT", [NSQ, NH, 128, PAST])
    cv_in = din("cv", [NSQ, PAST, 2048])

    y_out = dout("y_out", [NT, D])
    k_out = dout("k_out", [NT, 2048])
    v_out = dout("v_out", [NT, 2048])
    conv_out = dout("conv_out", [128, 160])

    qT_s = dscr("qT_s", [NH, 128, NT], BF16)
    kT_s = dscr("kT_s", [NH, 128, NT], BF16)
    kTh_s = dscr("kTh_s", [NH, 128, NHIST], BF16)
    v_s = dscr("v_s", [NT, 2048], BF16)
    vh_s = dscr("vh_s", [NHIST, 2048], BF16)
    sza_s = dscr("sza_s", [NH, 128, NT], F32)
    yb_s = dscr("yb_s", [16, 128, NT], BF16)
    sg_s = dscr("sg_s", [64, 128, NT], F32)
    ya_s = dscr("ya_s", [NH, 128, NT], BF16)
    mg_s = dscr("mg_s", [32, 128, NT], BF16)
    h_s = dscr("h_s", [NT, D], F32)

    w_in_v = w_in.rearrange("(c p) e -> p c e", p=128)
    w_a_v = w_a.rearrange("(c p) e -> p c e", p=128)
    w_b_v = w_b.rearrange("(c p) e -> p c e", p=128)
    w_o_v = w_o.rearrange("(c p) e -> p c e", p=128)

    with contextlib.ExitStack() as top:
        esems = {e: Sem(top.enter_context(nc.semaphore("es_" + e))) for e in ("tensor", "scalar", "vector", "gpsimd")}
        dpool = [Sem(top.enter_context(nc.semaphore("ds%d" % i))) for i in range(44)]
        P = Prog(nc, esems, dpool)

        uniq = [0]

        def sb(es, name, shape, dt):
            uniq[0] += 1
            return es.enter_context(nc.sbuf_tensor("%s_u%d" % (name, uniq[0]), list(shape), dt))

        def psbanks(es, n=8):
            uniq[0] += 1
            return [es.enter_context(nc.psum_tensor("pb%d_u%d" % (i, uniq[0]), [128, 512], F32)) for i in range(n)]

        def load_w(buf, dram_view, c0, ncol, nchunk, sem, waits):
            tok = None
            step = 8
            for q0 in range(0, nchunk, step):
                tok = P.dma("gpsimd", buf[:, q0:q0 + step, 0:ncol], dram_view[:, q0:q0 + step, c0:c0 + ncol], sem, waits)
            return tok

        def attn_factory(es, mode, PS, r_eng, mask_eng, yaT=None):
            cst = sb(es, "cst", [128, C_END], BF16)
            flag_t = sb(es, "flag_t", [128, 1], F32)
            ntri_h = sb(es, "ntri_h", [128, 128], BF16)
            none_h = sb(es, "none_h", [128, 128], BF16)
            if mode == "p":
                qTb = [sb(es, "qTb%d" % i, [128, NPR], BF16) for i in range(2)]
                kTb = [sb(es, "kTb%d" % i, [128, 2048], BF16) for i in range(2)]
                vbuf = [sb(es, "vbuf%d" % i, [128, 16, 128], BF16) for i in range(2)]
                szb = [sb(es, "szb%d" % i, [128, NPR], F32) for i in range(2)]
                yast = [sb(es, "yast%d" % i, [128, 512], BF16) for i in range(2)]
            else:
                ckb = [sb(es, "ckb%d" % i, [128, 8, PAST], BF16) for i in range(2)]
                cvb = [sb(es, "cvb%d" % i, [128, 8, 1024], BF16) for i in range(2)]
                qsm = [sb(es, "qsm%d" % i, [128, 8, SQ], BF16) for i in range(2)]
                ksm = [sb(es, "ksm%d" % i, [128, 8, SQ], BF16) for i in range(2)]
                vsm = [sb(es, "vsm%d" % i, [64, 1024], BF16) for i in range(2)]
                szs = [sb(es, "szs%d" % i, [128, 8, SQ], F32) for i in range(2)]
            e_t = [sb(es, "e_t%d" % i, [128, 512], F32) for i in range(2)]
            sp_t = [sb(es, "sp_t%d" % i, [128, 512], BF16) for i in range(3)]
            W_t = [sb(es, "W_t%d" % i, [128, 512], BF16) for i in range(2)]
            Rrow = [sb(es, "Rrow%d" % i, [32, 512], BF16) for i in range(2)]
            xc_t = [sb(es, "xc_t%d" % i, [128, 512], F32) for i in range(2)]
            ones_row = sb(es, "ones_row", [32, 128], BF16)
            one_t = sb(es, "one_t", [128, 1], F32)
            psA, psB, psO = PS
            nA, nB, nO = len(psA), len(psB), len(psO)
            s_c = P.dsem()
            s_q = [P.dsem(), P.dsem()]; s_k = [P.dsem(), P.dsem()]; s_v = [P.dsem(), P.dsem()]; s_z = [P.dsem(), P.dsem()]
            s_ck = [P.dsem(), P.dsem()]; s_cv = [P.dsem(), P.dsem()]; s_sm = [P.dsem(), P.dsem()]
            s_ya = [P.dsem(), P.dsem()]

            P.dma("sync", flag_t[:], flag_in[:], s_c)
            t_c = None
            for c0 in range(0, C_END, 1376):
                t_c = P.dma("gpsimd", cst[:, c0:c0 + 1376], consts_in[:, c0:c0 + 1376], s_c)
            t_f1 = P.op("vector", lambda e: e.tensor_scalar(out=ntri_h[:], in0=cst[:, C_TRI:C_TRI + 128], scalar1=flag_t[:, 0:1],
                                                             scalar2=None, op0=ALU.mult), [t_c])
            t_f2 = P.op("vector", lambda e: e.tensor_scalar(out=none_h[:], in0=cst[:, C_ONE:C_ONE + 128], scalar1=flag_t[:, 0:1],
                                                             scalar2=None, op0=ALU.mult), [t_c])
            P.op("vector", lambda e: e.memset(one_t[:], 1.0))
            P.op("vector", lambda e: e.memset(ones_row[:], 0.0))
            t_one = P.op("vector", lambda e: e.memset(ones_row[0:1, :], 1.0))
            ident_b = cst[:, C_ID:C_ID + 128]
            ntri = cst[:, C_TRI:C_TRI + 128]
            nones = cst[:, C_ONE:C_ONE + 128]

            jobs = []
            for h in range(NH if mode == "p" else 0):
                for T in range(2):
                    steps = []
                    for kb in range(4 * T + 3, -1, -1):
                        steps.append(dict(kind="p", kp=128, kcol=1024 + 128 * kb, vt=8 + kb,
                                          r=(kb - 4 * T) if kb >= 4 * T else None, flagged=False))
                    for bb in range(7, -1, -1):
                        steps.append(dict(kind="p", kp=128, kcol=128 * bb, vt=bb, r=None, flagged=True))
                    jobs.append(dict(kind="p", h=h, T=T, hb=h % 2, steps=steps))
            nsj = 0
            for s in range(NSQ if mode == "s" else 0):
                for g in range(2):
                    steps = [dict(kind="s", kp=64, own=True, r="s", flagged=False)]
                    for bb in range(7, -1, -1):
                        steps.append(dict(kind="s", kp=128, own=False, cb=bb, r=None, flagged=False))
                    jobs.append(dict(kind="s", s=s, g=g, sb=nsj % 2, steps=steps))
                    nsj += 1

            vh_v = vh_s.rearrange("(t p) e -> p t e", p=128)
            v_s_v = v_s.rearrange("(t p) e -> p t e", p=128)
            head_free = [[], []]
            samp_free = [[], []]
            head_tok = {}
            samp_tok = {}

            def load_head(h):
                hb = h % 2
                w = list(head_free[hb])
                tq = P.dma("sync", qTb[hb][:], qT_s[h][:, 0:NPR], s_q[hb], w)
                P.dma("sync", kTb[hb][:, 0:1024], kTh_s[h], s_k[hb], w)
                tk = P.dma("sync", kTb[hb][:, 1024:2048], kT_s[h][:, 0:NPR], s_k[hb], w)
                P.dma("sync", vbuf[hb][:, 0:8, :], vh_v[:, :, h * 128:(h + 1) * 128], s_v[hb], w)
                tv = P.dma("sync", vbuf[hb][:, 8:16, :], v_s_v[:, 0:8, h * 128:(h + 1) * 128], s_v[hb], w)
                tz = P.dma("sync", szb[hb][:], sza_s[h][:, 0:NPR], s_z[hb], w)
                head_tok[h] = dict(q=tq, k=tk, v=tv, z=tz)
                head_free[hb] = []

            def load_samp(ji):
                jb = jobs[ji]
                s, g, b2 = jb["s"], jb["g"], jb["sb"]
                w = list(samp_free[b2])
                tck = None
                for hh in range(0, 8, 2):
                    tck = P.dma("gpsimd", ckb[b2][:, hh:hh + 2, :],
                                ckT_in[s, g * 8 + hh:g * 8 + hh + 2].rearrange("h d k -> d h k"), s_ck[b2], w)
                tcv = None
                cvv = cv_in[s].rearrange("(t p) e -> p t e", p=128)
                for t0 in range(0, 8, 2):
                    tcv = P.dma("gpsimd", cvb[b2][:, t0:t0 + 2, :], cvv[:, t0:t0 + 2, g * 1024:(g + 1) * 1024], s_cv[b2], w)
                c0 = NPR + SQ * s
                P.dma("sync", qsm[b2][:], qT_s[g * 8:(g + 1) * 8, :, c0:c0 + SQ].rearrange("h d t -> d h t"), s_sm[b2], w)
                P.dma("sync", ksm[b2][:], kT_s[g * 8:(g + 1) * 8, :, c0:c0 + SQ].rearrange("h d t -> d h t"), s_sm[b2], w)
                P.dma("sync", vsm[b2][:], v_s[c0:c0 + SQ, g * 1024:(g + 1) * 1024], s_sm[b2], w)
                tsm = P.dma("sync", szs[b2][:], sza_s[g * 8:(g + 1) * 8, :, c0:c0 + SQ].rearrange("h d t -> d h t"), s_sm[b2], w)
                samp_tok[ji] = dict(ck=tck, cv=tcv, sm=tsm)
                samp_free[b2] = []

            flat = []
            for ji, jb in enumerate(jobs):
                n = len(jb["steps"])
                for si, st in enumerate(jb["steps"]):
                    st.update(job=ji, first=(si == 0), last=(si == n - 1))
                    flat.append(st)
            NS = len(flat)
            tok = [dict() for _ in range(NS)]
            psR_lastread = [None, None]
            psO_free = [None, None]
            yast_free = [None, None]
            ndiag = [0]
            last_evac = [None]

            def in_waits(st):
                jb = jobs[st["job"]]
                if jb["kind"] == "p":
                    ht = head_tok[jb["h"]]
                    return [ht["q"], ht["k"]], [ht["v"]], [ht["z"]]
                stt = samp_tok[st["job"]]
                return [stt["ck"], stt["sm"]], [stt["cv"], stt["sm"]], [stt["sm"]]

            def emit_S(s, st, ps, waits):
                jb = jobs[st["job"]]
                kp = st["kp"]
                if jb["kind"] == "p":
                    hb = jb["hb"]; T = jb["T"]
                    return [(lambda e, ps=ps, hb=hb, T=T, kc=st["kcol"]:
                             e.matmul(ps[:, :], lhsT=kTb[hb][:, kc:kc + 128], rhs=qTb[hb][:, 512 * T:512 * T + 512],
                                      start=True, stop=False, skip_group_check=True))]
                b2 = jb["sb"]
                fns = []
                for hh in range(8):
                    if st["own"]:
                        fns.append(lambda e, ps=ps, b2=b2, hh=hh:
                                   e.matmul(ps[0:64, hh * 64:(hh + 1) * 64], lhsT=ksm[b2][:, hh, :], rhs=qsm[b2][:, hh, :],
                                            start=(hh == 0), stop=False, skip_group_check=True))
                    else:
                        cb_ = st["cb"]
                        fns.append(lambda e, ps=ps, b2=b2, hh=hh, cb_=cb_:
                                   e.matmul(ps[:, hh * 64:(hh + 1) * 64], lhsT=ckb[b2][:, hh, cb_ * 128:(cb_ + 1) * 128],
                                            rhs=qsm[b2][:, hh, :], start=(hh == 0), stop=False, skip_group_check=True))
                return fns

            def stage1(s):
                st = flat[s]; kp = st["kp"]
                wq, wv, wz = in_waits(st)
                fns = emit_S(s, st, psA[s % nA], None)
                w = wq + [tok[s - nA].get("EA") if s >= nA else None]
                t = None
                for i, f in enumerate(fns):
                    t = P.op("tensor", f, w if i == 0 else (), signal=(i == len(fns) - 1))
                tok[s]["PE1"] = t
                tea = P.op("scalar", (lambda e, s=s, kp=kp: e.activation(out=e_t[s % 2][0:kp, :], in_=psA[s % nA][0:kp, :],
                                                                         func=AF.Exp, scale=1.0)),
                           [t, tok[s - 2].get("W") if s >= 2 else None])
                tok[s]["EA"] = tea
                if st["r"] is not None:
                    if st["r"] == "s":
                        m = cst[0:64, C_SM01:C_SM01 + 512]
                    else:
                        m = cst[:, C_M01 + 512 * st["r"]:C_M01 + 512 * st["r"] + 512]
                    tea = P.op(mask_eng, (lambda e, s=s, kp=kp, m=m: e.tensor_tensor(out=e_t[s % 2][0:kp, :], in0=e_t[s % 2][0:kp, :],
                                                                                      in1=m, op=ALU.mult)), [tea, t_c])
                tok[s]["EM"] = tea
                tok[s]["L"] = P.op("scalar", (lambda e, s=s, kp=kp: e.activation(out=sp_t[s % 3][0:kp, :], in_=e_t[s % 2][0:kp, :],
                                                                                  func=AF.Ln, bias=one_t[0:kp, 0:1], scale=1.0)),
                                   [tea, t_one, tok[s - 3].get("PE2") if s >= 3 else None])

            def stage2(s):
                st = flat[s]; kp = st["kp"]
                ps = psB[s % nB]
                tri = (ntri_h if st["flagged"] else ntri)
                has_r = not st["first"]
                fns = [lambda e, ps=ps, s=s, kp=kp, tri=tri, lastb=(not has_r):
                       e.matmul(ps[0:kp, :], lhsT=tri[0:kp, 0:kp], rhs=sp_t[s % 3][0:kp, :], start=True, stop=lastb,
                                skip_group_check=True)]
                if has_r:
                    fns.append(lambda e, ps=ps, s=s, kp=kp:
                               e.matmul(ps[0:kp, :], lhsT=ones_row[0:32, 0:kp], rhs=Rrow[(s - 1) % 2][0:32, :], start=False, stop=True,
                                        skip_group_check=True))
                w = [tok[s]["L"], t_f1, t_one]
                if s >= nB:
                    w += [tok[s - nB].get("EB"), tok[s - nB].get("R")]
                if has_r:
                    w.append(tok[s - 1].get("R"))
                t = None
                for i, f in enumerate(fns):
                    t = P.op("tensor", f, w if i == 0 else (), signal=(i == len(fns) - 1))
                tok[s]["PE2"] = t
                tok[s]["EB"] = P.op("scalar", (lambda e, s=s, kp=kp, ps=ps: e.activation(out=xc_t[s % 2][0:kp, :], in_=ps[0:kp, :],
                                                                                         func=AF.Exp, scale=1.0)),
                                    [t, tok[s - 2].get("W") if s >= 2 else None])
                if not st["last"]:
                    tok[s]["R"] = P.op("vector", (lambda e, s=s, ps=ps: e.tensor_copy(out=Rrow[s % 2][0:32, :], in_=ps[0:32, :])),
                                       [t, tok[s]["EB"]])
                tok[s]["W"] = P.op("vector", (lambda e, s=s, kp=kp: e.tensor_tensor(out=W_t[s % 2][0:kp, :], in0=e_t[s % 2][0:kp, :],
                                                                                    in1=xc_t[s % 2][0:kp, :], op=ALU.mult)),
                                   [tok[s]["EB"], tok[s]["EM"], tok[s - 2].get("PE3") if s >= 2 else None])

            def stage3(s):
                st = flat[s]; kp = st["kp"]
                ji = st["job"]; jb = jobs[ji]; jr = ji % nO
                wq, wv, wz = in_waits(st)
                fns = []
                if jb["kind"] == "p":
                    hb = jb["hb"]
                    fns.append(lambda e, s=s, hb=hb, vt=st["vt"], jr=jr, first=st["first"], last=st["last"]:
                               e.matmul(psO[jr][:, :], lhsT=vbuf[hb][:, vt, :], rhs=W_t[s % 2][:, :], start=first, stop=last, skip_group_check=True))
                else:
                    b2 = jb["sb"]
                    for hh in range(8):
                        if st["own"]:
                            fns.append(lambda e, s=s, b2=b2, hh=hh, jr=jr, first=st["first"], last=st["last"]:
                                       e.matmul(psO[jr][:, hh * 64:(hh + 1) * 64], lhsT=vsm[b2][0:64, hh * 128:(hh + 1) * 128],
                                                rhs=W_t[s % 2][0:64, hh * 64:(hh + 1) * 64], start=(first and hh == 0), stop=last, skip_group_check=True))
                        else:
                            fns.append(lambda e, s=s, b2=b2, hh=hh, jr=jr, cb_=st["cb"], first=st["first"], last=st["last"]:
                                       e.matmul(psO[jr][:, hh * 64:(hh + 1) * 64], lhsT=cvb[b2][:, cb_, hh * 128:(hh + 1) * 128],
                                                rhs=W_t[s % 2][:, hh * 64:(hh + 1) * 64], start=(first and hh == 0), stop=last, skip_group_check=True))
                w = wv + [tok[s]["W"], psO_free[jr] if st["first"] else None]
                t = None
                for i, f in enumerate(fns):
                    t = P.op("tensor", f, w if i == 0 else (), signal=(i == len(fns) - 1))
                tok[s]["PE3"] = t
                if st["last"]:
                    yb2 = ji % 2
                    if jb["kind"] == "p":
                        hb = jb["hb"]; T = jb["T"]; h = jb["h"]
                        te = P.op("vector", (lambda e, jr=jr, hb=hb, T=T, yb2=yb2:
                                             e.tensor_tensor(out=yast[yb2][:, :], in0=psO[jr][:, :],
                                                             in1=szb[hb][:, 512 * T:512 * T + 512], op=ALU.mult)),
                                  [t] + wz + [yast_free[yb2]])
                        yast_free[yb2] = P.dma("sync", ya_s[h][:, 512 * T:512 * T + 512], yast[yb2][:, :], s_ya[yb2], [te])
                        head_free[hb] = [t, te, tok[s]["PE2"]]
                    else:
                        b2 = jb["sb"]; s_ = jb["s"]; g = jb["g"]
                        c0 = NPR + SQ * s_
                        te = P.op("vector", (lambda e, jr=jr, b2=b2, g=g, c0=c0:
                                             e.tensor_tensor(out=yaT[:, g * 8:(g + 1) * 8, c0:c0 + SQ],
                                                             in0=psO[jr][:, :].rearrange("p (h t) -> p h t", t=SQ),
                                                             in1=szs[b2][:], op=ALU.mult)),
                                  [t] + wz)
                        samp_free[b2] = [t, te, tok[s]["PE2"]]
                        last_evac[0] = te
                    psO_free[jr] = te

            if mode == "p":
                load_head(0)
                load_head(1)
            else:
                load_samp(0)
                load_samp(1)
            tcks = [0]

            def tick():
                tck = tcks[0]
                if tck >= NS + 2:
                    return False
                tcks[0] += 1
                if tck < NS:
                    stage1(tck)
                if 0 <= tck - 1 < NS:
                    stage2(tck - 1)
                if 0 <= tck - 2 < NS:
                    s3 = tck - 2
                    stage3(s3)
                    st3 = flat[s3]
                    if st3["last"]:
                        jb = jobs[st3["job"]]
                        if jb["kind"] == "p" and jb["T"] == 1:
                            if jb["h"] + 2 < NH:
                                load_head(jb["h"] + 2)
                        elif jb["kind"] == "s":
                            nxt = st3["job"] + 2
                            if nxt < len(jobs):
                                load_samp(nxt)
                return True

            return tick, last_evac

        es_x = contextlib.ExitStack()
        xT = sb(es_x, "xT", [128, 32, NTX], BF16)
        s_xo = P.dsem()
        with contextlib.ExitStack() as es:
            xTh = sb(es, "xTh", [128, 32, NHIST], BF16)
            wbuf = [sb(es, "wbuf%d" % i, [128, 32, 256], BF16) for i in range(2)]
            kst = [sb(es, "kst%d" % i, [128, NHIST], BF16) for i in range(2)]
            vst = [sb(es, "vst%d" % i, [128, 8, 256], BF16) for i in range(2)]
            bias_t = sb(es, "bias_t", [128, 192], F32)
            bv_t = sb(es, "bv_t", [128, 2048], F32)
            bvf_t = sb(es, "bvf_t", [128, 2048], F32)
            flag_t = sb(es, "flag_t", [128, 1], F32)
            pb = psbanks(es, 6)
            s_xg = [P.dsem() for _ in range(8)]; s_c = P.dsem(); s_w = [P.dsem(), P.dsem()]
            s_ks = [P.dsem(), P.dsem()]; s_vs = [P.dsem(), P.dsem()]

            t_c = P.dma("sync", bias_t[:], bias_fm[:], s_c)
            t_c = P.dma("sync", bv_t[:], bv_bc[:], s_c)
            t_c = P.dma("sync", flag_t[:], flag_in[:], s_c)
            t_xg = [None] * 8
            t_x = None
            t_bvf = P.op("vector", lambda e: e.tensor_scalar(out=bvf_t[:], in0=bv_t[:], scalar1=flag_t[:, 0:1],
                                                              scalar2=None, op0=ALU.mult), [t_c])

            bank_free = {}
            wfree = [None, None]
            st_free_k = [None, None]
            st_free_v = [None, None]
            nw = 0
            item = 0
            for wt in range(8):
                b = nw % 2
                t_w = load_w(wbuf[b], w_in_v, 2048 + wt * 256, 256, 32, s_w[b], [wfree[b]])
                nw += 1
                if wt == 0:
                    for gq in range(8):
                        t_xg[gq] = P.dma("gpsimd", xTh[:, 4 * gq:4 * gq + 4, :], xT_hist[:, 4 * gq:4 * gq + 4, :], s_xg[gq])
                if wt == 2:
                    for q0 in range(0, 32, 4):
                        t_xo = P.dma("gpsimd", xT[:, q0:q0 + 4, :], xT_own[:, q0:q0 + 4, :], s_xo)
                last_mm = None
                for half in range(2):
                    h = 2 * wt + half
                    banks = [0, 1] if item % 2 == 0 else [3, 4]
                    waits = [t_w] + [bank_free.get(bk) for bk in banks]
                    for c in range(32):
                        for ti in range(2):
                            last = (c == 31 and ti == 1)
                            last_mm = P.op("tensor",
                                           (lambda e, bk=banks[ti], b=b, c=c, half=half, ti=ti:
                                            e.matmul(pb[bk][:, :], lhsT=wbuf[b][:, c, half * 128:(half + 1) * 128],
                                                     rhs=xTh[:, c, ti * 512:(ti + 1) * 512],
                                                     start=(c == 0), stop=(c == 31))),
                                           (waits if (c == 0 and ti == 0) else []) + ([t_xg[c // 4]] if (item == 0 and c % 4 == 0 and ti == 0) else []),
                                           signal=last)
                    kb = item % 2
                    tk = None
                    for ti in range(2):
                        tk = P.op("vector",
                                  (lambda e, bk=banks[ti], kb=kb, ti=ti, h=h:
                                   e.tensor_scalar(out=kst[kb][:, ti * 512:(ti + 1) * 512], in0=pb[bk][:, :],
                                                   scalar1=bias_t[:, 16 + h:17 + h], scalar2=None, op0=ALU.add)),
                                  [last_mm, st_free_k[kb], t_c])
                        bank_free[banks[ti]] = tk
                    st_free_k[kb] = P.dma("sync", kTh_s[h], kst[kb][:], s_ks[kb], [tk])
                    item += 1
                wfree[b] = last_mm
            vh_v = vh_s.rearrange("(t p) e -> p t e", p=128)
            nb = 0
            for cs in range(8):
                b = nw % 2
                t_w = load_w(wbuf[b], w_in_v, 4096 + cs * 256, 256, 32, s_w[b], [wfree[b]])
                nw += 1
                vb = cs % 2
                last_mm = None
                tv = None
                for tt in range(8):
                    bk = nb % 6
                    nb += 1
                    waits = [t_w, bank_free.get(bk)]
                    for c in range(32):
                        last_mm = P.op("tensor",
                                       (lambda e, bk=bk, b=b, c=c, tt=tt:
                                        e.matmul(pb[bk][:, 0:256], lhsT=xTh[:, c, tt * 128:(tt + 1) * 128],
                                                 rhs=wbuf[b][:, c, :], start=(c == 0), stop=(c == 31))),
                                       waits if c == 0 else (), signal=(c == 31))
                    tv = P.op("vector",
                              (lambda e, bk=bk, vb=vb, tt=tt, cs=cs:
                               e.scalar_tensor_tensor(out=vst[vb][:, tt, :], in0=pb[bk][:, 0:256], scalar=flag_t[:, 0:1],
                                                      in1=bvf_t[:, cs * 256:(cs + 1) * 256], op0=ALU.mult, op1=ALU.add)),
                              [last_mm, t_bvf, st_free_v[vb] if tt == 0 else None])
                    bank_free[bk] = tv
                st_free_v[vb] = P.dma("sync", vh_v[:, :, cs * 256:(cs + 1) * 256], vst[vb][:], s_vs[vb], [tv])
                wfree[b] = last_mm
            P.run_block()

        with contextlib.ExitStack() as es:
            wbuf = [sb(es, "wbuf%d" % i, [128, 32, 256], BF16) for i in range(2)]
            bias_t = sb(es, "bias_t", [128, 192], F32)
            bv_t = sb(es, "bv_t", [128, 2048], F32)
            flag_t = sb(es, "flag_t", [128, 1], F32)
            ident_f = sb(es, "ident_f", [128, 128], F32)
            fst = [sb(es, "fst%d" % i, [128, NTX], F32) for i in range(2)]
            bst = [sb(es, "bst%d" % i, [128, NT], BF16) for i in range(2)]
            ktst = [sb(es, "ktst%d" % i, [128, 10, 128], F32) for i in range(2)]
            vfst = [sb(es, "vfst%d" % i, [128, 10, 256], F32) for i in range(1)]
            vbst = [sb(es, "vbst%d" % i, [128, 10, 256], BF16) for i in range(1)]
            pb = psbanks(es, 8)
            s_x = P.dsem(); s_c = P.dsem(); s_w = [P.dsem(), P.dsem()]
            s_f = [P.dsem(), P.dsem()]; s_b = [P.dsem(), P.dsem()]; s_kt = [P.dsem(), P.dsem()]
            s_vf = [P.dsem(), P.dsem()]; s_vb = [P.dsem(), P.dsem()]; s_co = P.dsem()

            t_c = P.dma("sync", bias_t[:], bias_fm[:], s_c)
            P.dma("sync", bv_t[:], bv_bc[:], s_c)
            P.dma("sync", flag_t[:], flag_in[:], s_c)
            t_c = P.dma("sync", ident_f[:], consts_in[:, C_ID:C_ID + 128], s_c)
            t_x = t_xo

            bank_free = {}
            wfree = [None, None]
            state = {"nw": 0, "item": 0, "nb": 0}
            fst_free = [None, None]
            bst_free = [None, None]
            ktst_free = [None, None]
            pending_pe = []

            def next_w(c0):
                b = state["nw"] % 2
                state["nw"] += 1
                tw = load_w(wbuf[b], w_in_v, c0, 256, 32, s_w[b], [wfree[b]])
                return b, tw

            def mm_F(b, half, t_w):
                banks = [0, 1, 2] if state["item"] % 2 == 0 else [3, 4, 5]
                state["item"] += 1
                waits = [t_w, t_x] + [bank_free.get(bk) for bk in banks]
                last_mm = None
                for c in range(32):
                    for ti, (t0, n) in enumerate(TT3):
                        last = (c == 31 and ti == 2)
                        last_mm = P.op("tensor",
                                       (lambda e, bk=banks[ti], b=b, c=c, half=half, t0=t0, n=n:
                                        e.matmul(pb[bk][:, 0:n], lhsT=wbuf[b][:, c, half * 128:(half + 1) * 128],
                                                 rhs=xT[:, c, t0:t0 + n], start=(c == 0), stop=(c == 31))),
                                       waits if (c == 0 and ti == 0) else (), signal=last)
                wfree[b] = last_mm
                for f in pending_pe:
                    f()
                del pending_pe[:]
                return banks, last_mm

            def act_banks(banks, last_mm, dst, func, bcol, extra_waits, ncols3=TT3):
                tk = None
                for ti, (t0, n) in enumerate(ncols3):
                    tk = P.op("scalar",
                              (lambda e, bk=banks[ti], t0=t0, n=n:
                               e.activation(out=dst[:, t0:t0 + n], in_=pb[bk][:, 0:n], func=func,
                                            bias=bias_t[:, bcol:bcol + 1], scale=1.0)),
                              [last_mm, t_c] + list(extra_waits))
                    bank_free[banks[ti]] = tk
                return tk

            k_out_v = k_out.rearrange("(t p) e -> p t e", p=128)
            for wt in range(8):
                b, t_w = next_w(2048 + wt * 256)
                for half in range(2):
                    h = 2 * wt + half
                    banks, last_mm = mm_F(b, half, t_w)
                    i2 = h % 2
                    t_kf = act_banks(banks, last_mm, fst[i2], AF.Identity, 16 + h, [fst_free[i2]])
                    t_kb = P.op("vector", (lambda e, i2=i2: e.tensor_copy(out=bst[i2][:], in_=fst[i2][:, 0:NT])),
                                [t_kf, bst_free[i2]])
                    bst_free[i2] = P.dma("sync", kT_s[h], bst[i2][:], s_b[i2], [t_kb])

                    def k_transposes(i2=i2, h=h, t_kf=t_kf, t_kb=t_kb):
                        t_cp = None
                        t_tr = None
                        for g0 in range(0, 10, 4):
                            ng = min(4, 10 - g0)
                            bk = 6 + (g0 // 4) % 2
                            for j in range(ng):
                                tt = g0 + j
                                t_tr = P.op("tensor",
                                            (lambda e, bk=bk, j=j, tt=tt:
                                             e.transpose(out=pb[bk][:, j * 128:(j + 1) * 128],
                                                         in_=fst[i2][:, tt * 128:(tt + 1) * 128], identity=ident_f[:])),
                                            [t_kf, t_c, bank_free.get(bk)] if j == 0 else (), signal=(j == ng - 1))
                            t_cp = P.op("vector",
                                        (lambda e, bk=bk, g0=g0, ng=ng:
                                         e.tensor_copy(out=ktst[i2][:, g0:g0 + ng, :].rearrange("p a b -> p (a b)"),
                                                       in_=pb[bk][:, 0:ng * 128])),
                                        [t_tr, ktst_free[i2] if g0 == 0 else None])
                            bank_free[bk] = t_cp
                        ktst_free[i2] = P.dma("sync", k_out_v[:, :, h * 128:(h + 1) * 128], ktst[i2][:], s_kt[i2], [t_cp])
                        fst_free[i2] = t_cp
                    pending_pe.append(k_transposes)
                    fst_free[i2] = None

            for wt in range(8):
                b, t_w = next_w(0 + wt * 256)
                for half in range(2):
                    h = 2 * wt + half
                    banks, last_mm = mm_F(b, half, t_w)
                    i2 = h % 2
                    tq = None
                    for ti, (t0, n) in enumerate(TT3o):
                        tq = P.op("vector",
                                  (lambda e, bk=banks[ti], t0=t0, n=n, h=h, i2=i2:
                                   e.tensor_scalar(out=bst[i2][:, t0:t0 + n], in0=pb[bk][:, 0:n],
                                                   scalar1=bias_t[:, h:h + 1], scalar2=float(QSCALE),
                                                   op0=ALU.add, op1=ALU.mult)),
                                  [last_mm, t_c, bst_free[i2]])
                        bank_free[banks[ti]] = tq
                    bst_free[i2] = P.dma("sync", qT_s[h], bst[i2][:], s_b[i2], [tq])

            for wt in range(8):
                b, t_w = next_w(6144 + wt * 256)
                for half in range(2):
                    h = 2 * wt + half
                    banks, last_mm = mm_F(b, half, t_w)
                    i2 = h % 2
                    tz = act_banks(banks, last_mm, fst[i2], AF.Silu, 48 + h, [fst_free[i2]], TT3o)
                    fst_free[i2] = P.dma("sync", sza_s[h], fst[i2][:, 0:NT], s_f[i2], [tz])

            v_out_v = v_out.rearrange("(t p) e -> p t e", p=128)
            v_s_v = v_s.rearrange("(t p) e -> p t e", p=128)
            vf_free = [None, None]; vb_free = [None, None]
            for cs in range(8):
                b, t_w = next_w(4096 + cs * 256)
                vb = 0
                last_mm = None
                tv = tv2 = None
                for tt in range(10):
                    bk = state["nb"] % 6
                    state["nb"] += 1
                    waits = [t_w, t_x, bank_free.get(bk)]
                    for c in range(32):
                        last_mm = P.op("tensor",
                                       (lambda e, bk=bk, b=b, c=c, tt=tt:
                                        e.matmul(pb[bk][:, 0:256], lhsT=xT[:, c, tt * 128:(tt + 1) * 128],
                                                 rhs=wbuf[b][:, c, :], start=(c == 0), stop=(c == 31))),
                                       waits if c == 0 else (), signal=(c == 31))
                    if tt == 0:
                        for f in pending_pe:
                            f()
                        del pending_pe[:]
                    tv = P.op("vector", (lambda e, bk=bk, vb=vb, tt=tt, cs=cs:
                                         e.tensor_tensor(out=vfst[vb][:, tt, :], in0=pb[bk][:, 0:256],
                                                         in1=bv_t[:, cs * 256:(cs + 1) * 256], op=ALU.add)),
                              [last_mm, t_c, vf_free[vb] if tt == 0 else None])
                    bank_free[bk] = tv
                    tv2 = P.op("scalar", (lambda e, vb=vb, tt=tt: e.activation(out=vbst[vb][:, tt, :], in_=vfst[vb][:, tt, :],
                                                                                func=AF.Identity, scale=1.0)),
                               [tv, vb_free[vb] if tt == 0 else None])
                wfree[b] = last_mm
                vf_free[vb] = P.dma("sync", v_out_v[:, :, cs * 256:(cs + 1) * 256], vfst[vb][:], s_vf[vb], [tv, tv2])
                vb_free[vb] = P.dma("sync", v_s_v[:, :, cs * 256:(cs + 1) * 256], vbst[vb][:], s_vb[vb], [tv2])
            P.run_block()

        with contextlib.ExitStack() as es:
            wbuf = [sb(es, "wbuf%d" % i, [128, 32, 256], BF16) for i in range(2)]
            bias_t = sb(es, "bias_t", [128, 192], F32)
            flag_t = sb(es, "flag_t", [128, 1], F32)
            cw_t = sb(es, "cw_t", [128, 48], F32)
            cb_t = sb(es, "cb_t", [128, 16], F32)
            sc_t = sb(es, "sc_t", [128, 16, 4, 2], F32)
            fst = [sb(es, "fst%d" % i, [128, NT], F32) for i in range(2)]
            bst = [sb(es, "bst%d" % i, [128, NT], BF16) for i in range(2)]
            t1 = [sb(es, "t1_0", [128, NTX], F32)]
            ue = [sb(es, "ue0", [128, 1290], F32)]
            acc = [sb(es, "acc0", [128, NT], F32)]
            s1 = [sb(es, "s1_0", [128, NT], F32)]
            tmp2 = sb(es, "tmp2", [128, 2], F32)
            cout = sb(es, "cout", [128, 16, 10], F32)
            pb = psbanks(es, 8)
            s_c = P.dsem(); s_w = [P.dsem(), P.dsem()]
            s_f = [P.dsem(), P.dsem()]; s_b = [P.dsem(), P.dsem()]; s_co = P.dsem()

            t_c = P.dma("sync", bias_t[:], bias_fm[:], s_c)
            P.dma("sync", flag_t[:], flag_in[:], s_c)
            P.dma("sync", cw_t[:], cw_fm[:], s_c)
            P.dma("sync", cb_t[:], cb_fm[:], s_c)
            t_c = P.dma("sync", sc_t[:].rearrange("p a b c -> p (a b c)"), sc_fm[:], s_c)
            t_x = t_xo

            t_w_pre = load_w(wbuf[0], w_in_v, 16384, 256, 32, s_w[0], [])
            nbias_t = sb(es, "nbias_t", [128, 192], F32)
            one_m = sb(es, "one_m", [128, 1], F32)
            P.op("vector", lambda e: e.memset(one_m[:], 1.0))
            t_nb = P.op("vector", lambda e: e.tensor_scalar(out=nbias_t[:], in0=bias_t[:], scalar1=-1.0, scalar2=None, op0=ALU.mult), [t_c])
            tick, _ = attn_factory(es, "p", ([pb[3], pb[4]], [pb[5], pb[6]], [pb[7]]), "scalar", "vector")

            bank_free = {}
            wfree = [None, None]
            state = {"nw": 0, "mm": 0}
            fst_free = [None, None]
            bst_free = [None, None]

            def next_w(c0, ncol=256):
                b = state["nw"] % 2
                state["nw"] += 1
                tw = load_w(wbuf[b], w_in_v, c0, ncol, 32, s_w[b], [wfree[b]])
                return b, tw

            def mm_F(b, half, t_w, tiles=TT3):
                lm = []
                for ti, (t0, n) in enumerate(tiles):
                    waits = [t_w, t_x, bank_free.get(ti)]
                    tk = None
                    for c in range(32):
                        tk = P.op("tensor",
                                  (lambda e, ti=ti, b=b, c=c, half=half, t0=t0, n=n:
                                   e.matmul(pb[ti][:, 0:n], lhsT=wbuf[b][:, c, half * 128:(half + 1) * 128],
                                            rhs=xT[:, c, t0:t0 + n], start=(c == 0), stop=(c == 31))),
                                  waits if c == 0 else (), signal=(c == 31))
                        state["mm"] += n
                        if state["mm"] >= 11300:
                            state["mm"] -= 11300
                            tick()
                    lm.append(tk)
                wfree[b] = lm[-1]
                return lm

            def act_banks(lm, dst, func, bcol, extra_waits, tiles):
                tk = None
                for ti, (t0, n) in enumerate(tiles):
                    tk = P.op("scalar",
                              (lambda e, ti=ti, t0=t0, n=n:
                               e.activation(out=dst[:, t0:t0 + n], in_=pb[ti][:, 0:n], func=func,
                                            bias=bias_t[:, bcol:bcol + 1], scale=1.0)),
                              [lm[ti], t_c] + list(extra_waits))
                    bank_free[ti] = tk
                return tk

            sig_tile_tok = [None, None, None]

            def sig_banks(lm, dst, bcol, extra_waits, tiles):
                tk = None
                for ti, (t0, n) in enumerate(tiles):
                    t_a = P.op("scalar",
                               (lambda e, ti=ti, t0=t0, n=n:
                                e.activation(out=dst[:, t0:t0 + n], in_=pb[ti][:, 0:n], func=AF.Exp,
                                             bias=nbias_t[:, bcol:bcol + 1], scale=-1.0)),
                               [lm[ti], t_nb] + list(extra_waits))
                    t_b = P.op("scalar",
                               (lambda e, t0=t0, n=n:
                                e.activation(out=dst[:, t0:t0 + n], in_=dst[:, t0:t0 + n], func=AF.Ln,
                                             bias=one_m[:, 0:1], scale=1.0)), [t_a])
                    tk = P.op("scalar",
                              (lambda e, t0=t0, n=n:
                               e.activation(out=dst[:, t0:t0 + n], in_=dst[:, t0:t0 + n], func=AF.Exp, scale=-1.0)), [t_b])
                    bank_free[ti] = t_a
                    sig_tile_tok[ti] = tk
                return tk

            for wt in range(32):
                if wt == 0:
                    b, t_w = 0, t_w_pre
                    state["nw"] = 1
                else:
                    b, t_w = next_w(16384 + wt * 256)
                for half in range(2):
                    j = 2 * wt + half
                    lm = mm_F(b, half, t_w, TT3o)
                    i2 = j % 2
                    tz = sig_banks(lm, fst[i2], 128 + j, [fst_free[i2]], TT3o)
                    fst_free[i2] = P.dma("sync", sg_s[j], fst[i2][:], s_f[i2], [tz])

            conv_free = None
            t_cout = None
            p = 0
            for c in range(16):
                ue_s = ue[p][:, 1026:1290].rearrange("p (s t) -> p s t", t=66)
                acc_s = acc[p][:, 1024:1280].rearrange("p (s t) -> p s t", t=64)
                b, t_w = next_w(10240 + c * 128, 128)
                lm = mm_F(b, 0, t_w, TT3)
                t_t1 = act_banks(lm, t1[p], AF.Identity, 80 + c, [conv_free], TT3)
                b, t_w = next_w(12288 + c * 128, 128)
                lm = mm_F(b, 0, t_w, TT3)
                bcol = 96 + c
                ta = P.op("vector", (lambda e, bcol=bcol:
                                     e.scalar_tensor_tensor(out=ue[p][:, 2:514], in0=pb[0][:, 0:512],
                                                            scalar=bias_t[:, bcol:bcol + 1], in1=t1[p][:, 0:512],
                                                            op0=ALU.add, op1=ALU.mult)), [t_t1, lm[0], t_c, conv_free])
                bank_free[0] = ta
                tb = P.op("vector", (lambda e, bcol=bcol:
                                     e.scalar_tensor_tensor(out=ue[p][:, 514:1026], in0=pb[1][:, 0:512],
                                                            scalar=bias_t[:, bcol:bcol + 1], in1=t1[p][:, 512:1024],
                                                            op0=ALU.add, op1=ALU.mult)), [t_t1, lm[1]])
                bank_free[1] = tb
                tc_ = P.op("vector", (lambda e, bcol=bcol, ue_s=ue_s:
                                      e.scalar_tensor_tensor(out=ue_s[:, :, 2:66],
                                                             in0=pb[2][:, 0:256].rearrange("p (s t) -> p s t", t=64),
                                                             scalar=bias_t[:, bcol:bcol + 1],
                                                             in1=t1[p][:, 1024:1280].rearrange("p (s t) -> p s t", t=64),
                                                             op0=ALU.add, op1=ALU.mult)), [t_t1, lm[2]])
                td = P.op("vector", (lambda e, bcol=bcol:
                                     e.scalar_tensor_tensor(out=tmp2[:, :], in0=pb[2][:, 256:258],
                                                            scalar=bias_t[:, bcol:bcol + 1], in1=t1[p][:, 1280:1282],
                                                            op0=ALU.add, op1=ALU.mult)), [t_t1, lm[2]])
                bank_free[2] = td
                te = P.op("vector", (lambda e: e.tensor_scalar(out=ue[p][:, 0:2], in0=tmp2[:, :],
                                                                scalar1=flag_t[:, 0:1], scalar2=None, op0=ALU.mult)), [td])
                tf = P.op("vector", (lambda e, c=c, ue_s=ue_s: e.tensor_copy(out=ue_s[:, :, 0:2], in_=sc_t[:, c, :, :])), [t_c])
                w0 = cw_t[:, 3 * c:3 * c + 1]; w1 = cw_t[:, 3 * c + 1:3 * c + 2]; w2 = cw_t[:, 3 * c + 2:3 * c + 3]
                cbv = cb_t[:, c:c + 1]
                t3 = P.op("vector", (lambda e, w0=w0, cbv=cbv:
                                     e.tensor_scalar(out=acc[p][:, 0:1024], in0=ue[p][:, 0:1024], scalar1=w0, scalar2=cbv,
                                                     op0=ALU.mult, op1=ALU.add)), [ta, tb, te])
                t3 = P.op("vector", (lambda e, w1=w1:
                                     e.scalar_tensor_tensor(out=acc[p][:, 0:1024], in0=ue[p][:, 1:1025], scalar=w1,
                                                            in1=acc[p][:, 0:1024], op0=ALU.mult, op1=ALU.add)), [t3])
                t3 = P.op("vector", (lambda e, w2=w2:
                                     e.scalar_tensor_tensor(out=acc[p][:, 0:1024], in0=ue[p][:, 2:1026], scalar=w2,
                                                            in1=acc[p][:, 0:1024], op0=ALU.mult, op1=ALU.add)), [t3])
                t4 = P.op("vector", (lambda e, ue_s=ue_s, acc_s=acc_s, w0=w0, cbv=cbv:
                                     e.tensor_scalar(out=acc_s, in0=ue_s[:, :, 0:64], scalar1=w0, scalar2=cbv,
                                                     op0=ALU.mult, op1=ALU.add)), [tc_, tf])
                t4 = P.op("vector", (lambda e, ue_s=ue_s, acc_s=acc_s, w1=w1:
                                     e.scalar_tensor_tensor(out=acc_s, in0=ue_s[:, :, 1:65], scalar=w1, in1=acc_s,
                                                            op0=ALU.mult, op1=ALU.add)), [t4])
                t4 = P.op("vector", (lambda e, ue_s=ue_s, acc_s=acc_s, w2=w2:
                                     e.scalar_tensor_tensor(out=acc_s, in0=ue_s[:, :, 2:66], scalar=w2, in1=acc_s,
                                                            op0=ALU.mult, op1=ALU.add)), [t4])
                P.op("vector", (lambda e, c=c: e.tensor_copy(out=cout[:, c, 0:2], in_=ue[p][:, 1024:1026])), [ta, tb])
                t_cout = P.op("vector", (lambda e, c=c, ue_s=ue_s:
                                         e.tensor_copy(out=cout[:, c, 2:10].rearrange("p (s t) -> p s t", t=2),
                                                       in_=ue_s[:, :, 64:66])), [tc_])
                t_acc = t_cout
                b, t_w = next_w(14336 + c * 128, 128)
                lm = mm_F(b, 0, t_w, TT3o)
                t_sg = sig_banks(lm, s1[p], 112 + c, [conv_free], TT3o)
                t_s1 = None
                for ti, (t0, n) in enumerate(TT3o):
                    t_s1 = P.op("vector", (lambda e, ti=ti, t0=t0, n=n, bc=112 + c:
                                           e.scalar_tensor_tensor(out=s1[p][:, t0:t0 + n], in0=pb[ti][:, 0:n],
                                                                  scalar=bias_t[:, bc:bc + 1], in1=s1[p][:, t0:t0 + n],
                                                                  op0=ALU.add, op1=ALU.mult)), [sig_tile_tok[ti], t_c])
                    bank_free[ti] = t_s1
                b, t_w = next_w(8192 + c * 128, 128)
                lm = mm_F(b, 0, t_w, TT3o)
                bcol = 64 + c
                ty = None
                for ti, (t0, n) in enumerate(TT3o):
                    ty = P.op("vector", (lambda e, ti=ti, t0=t0, n=n, bcol=bcol:
                                         e.scalar_tensor_tensor(out=s1[p][:, t0:t0 + n], in0=pb[ti][:, 0:n],
                                                                scalar=bias_t[:, bcol:bcol + 1], in1=s1[p][:, t0:t0 + n],
                                                                op0=ALU.add, op1=ALU.mult)), [lm[ti], t_s1, t_c])
                    bank_free[ti] = ty
                i2 = c % 2
                ty = P.op("vector", (lambda e, i2=i2: e.tensor_tensor(out=bst[i2][:], in0=s1[p][:], in1=acc[p][:], op=ALU.mult)),
                          [ty, t_acc, bst_free[i2]])
                conv_free = ty
                bst_free[i2] = P.dma("sync", yb_s[c], bst[i2][:], s_b[i2], [ty])
            P.dma("sync", conv_out[:], cout[:].rearrange("p a b -> p (a b)"), s_co, [t_cout])
            while tick():
                pass
            P.run_block()
        es_x.close()

        es_wo = contextlib.ExitStack()
        wob0 = sb(es_wo, "wob0", [128, 32, 512], BF16)
        es_y = contextlib.ExitStack()
        yaT = sb(es_y, "yaT", [128, 16, NT], BF16)
        s_yb = P.dsem()
        with contextlib.ExitStack() as es:
            pb = psbanks(es, 8)
            tick, last_evac = attn_factory(es, "s", ([pb[0], pb[1]], [pb[2], pb[3]], [pb[6], pb[7]]),
                                           "vector", "gpsimd", yaT=yaT)
            for hh in range(16):
                P.dma("sync", yaT[:, hh, 0:NPR], ya_s[hh][:, 0:NPR], s_yb)
            while tick():
                pass
            if debug:
                s_dbg = P.dsem()
                for hh in range(16):
                    P.dma("sync", ya_s[hh][:, NPR:NT], yaT[:, hh, NPR:NT], s_dbg, [last_evac[0]])
            P.run_block()


        es_y2 = contextlib.ExitStack()
        ybT = sb(es_y2, "ybT", [128, 16, NT], BF16)
        with contextlib.ExitStack() as es:
            wab = [sb(es, "wab%d" % i, [128, 16, 256], BF16) for i in range(4)]
            sgb = [sb(es, "sgb%d" % i, [128, NT], F32) for i in range(4)]
            ta_t = [sb(es, "ta_t%d" % i, [128, NT], F32) for i in range(2)]
            tb_t = sb(es, "tb_t", [128, NT], F32)
            mst = [sb(es, "mst%d" % i, [128, NT], BF16) for i in range(2)]
            pb = psbanks(es, 6)
            s_y = P.dsem(); s_w = [P.dsem() for _ in range(4)]; s_g = [P.dsem() for _ in range(4)]
            s_m = [P.dsem(), P.dsem()]
            t_y = None
            bank_free = {}
            wfree = [None] * 4
            sg_free = [None] * 4
            mst_free = [None, None]
            item = 0
            wtok = {}
            for jp in range(16):
                for br in range(2):
                    wi = (2 * jp + br) % 4
                    wtok[(jp, br)] = load_w(wab[wi], (w_a_v if br == 0 else w_b_v), jp * 256, 256, 16, s_w[wi], [wfree[wi]])
                for half in range(2):
                    j = 2 * jp + half
                    t_a = None
                    for br in range(2):
                        wi = (2 * jp + br) % 4
                        gi = (2 * j + br) % 4
                        t_g = P.dma("sync", sgb[gi][:], sg_s[j + 32 * br], s_g[gi], [sg_free[gi]])
                        if jp == 0 and half == 0 and br == 0:
                            for hh in range(16):
                                t_y = P.dma("sync", ybT[:, hh, :], yb_s[hh], s_y)
                        banks = [0, 1, 2] if item % 2 == 0 else [3, 4, 5]
                        item += 1
                        src = yaT if br == 0 else ybT
                        waits = [wtok[(jp, br)], (t_y if br == 1 else None)] + [bank_free.get(bk) for bk in banks]
                        last_mm = None
                        for c in range(16):
                            for ti, (t0, n) in enumerate(TT3o):
                                last = (c == 15 and ti == 2)
                                last_mm = P.op("tensor",
                                               (lambda e, bk=banks[ti], wi=wi, c=c, half=half, t0=t0, n=n, src=src:
                                                e.matmul(pb[bk][:, 0:n], lhsT=wab[wi][:, c, half * 128:(half + 1) * 128],
                                                         rhs=src[:, c, t0:t0 + n], start=(c == 0), stop=(c == 15))),
                                               waits if (c == 0 and ti == 0) else (), signal=last)
                        if half == 1:
                            wfree[wi] = last_mm
                        j2 = j % 2
                        dst = ta_t[j2] if br == 0 else tb_t
                        tk = None
                        for ti, (t0, n) in enumerate(TT3o):
                            tk = P.op("vector", (lambda e, bk=banks[ti], t0=t0, n=n, gi=gi, dst=dst:
                                                 e.tensor_tensor(out=dst[:, t0:t0 + n], in0=pb[bk][:, 0:n], in1=sgb[gi][:, t0:t0 + n],
                                                                 op=ALU.mult)), [last_mm, t_g, t_a if br == 1 else mst_free[j2]])
                            bank_free[banks[ti]] = tk
                        sg_free[gi] = tk
                        if br == 0:
                            t_a = tk
                        else:
                            tm = P.op("vector", (lambda e, j2=j2: e.tensor_tensor(out=mst[j2][:], in0=ta_t[j2][:], in1=tb_t[:], op=ALU.add)),
                                      [tk, t_a, mst_free[j2]])
                            mst_free[j2] = P.dma("sync", mg_s[j], mst[j2][:], s_m[j2], [tm])
                            tb_guard = tm
            s_wo0 = P.dsem()
            load_w(wob0, w_o_v, 0, 512, 32, s_wo0, [])
            P.run_block()
        es_y2.close()
        es_y.close()

        with contextlib.ExitStack() as es:
            mT = sb(es, "mT", [128, 32, NT], BF16)
            wob = [wob0, sb(es, "wob1", [128, 32, 512], BF16)]
            xt_t = [sb(es, "xt_t%d" % i, [128, 512], F32) for i in range(3)]
            hst = [sb(es, "hst%d" % i, [128, 512], F32) for i in range(3)]
            pb = psbanks(es, 6)
            s_m = P.dsem(); s_w = [P.dsem(), P.dsem()]; s_x = [P.dsem() for _ in range(3)]; s_h = [P.dsem() for _ in range(3)]
            t_m = None
            for cc in range(32):
                t_m = P.dma("sync", mT[:, cc, :], mg_s[cc], s_m)
            bank_free = {}
            wfree = [None, None]
            x_free = [None] * 3
            h_free = [None] * 3
            it = 0
            for n8 in range(8):
                b = n8 % 2
                if n8 == 0:
                    t_w = None
                else:
                    t_w = load_w(wob[b], w_o_v, n8 * 512, 512, 32, s_w[b], [wfree[b]])
                last_mm = None
                for tt in range(10):
                    bk = it % 6
                    xi = it % 3
                    it += 1
                    t_xl = P.dma("sync", xt_t[xi][:], x_tok[tt * 128:(tt + 1) * 128, n8 * 512:(n8 + 1) * 512], s_x[xi], [x_free[xi]])
                    waits = [t_w, t_m, bank_free.get(bk)]
                    for c in range(32):
                        last_mm = P.op("tensor",
                                       (lambda e, bk=bk, b=b, c=c, tt=tt:
                                        e.matmul(pb[bk][:, :], lhsT=mT[:, c, tt * 128:(tt + 1) * 128], rhs=wob[b][:, c, :],
                                                 start=(c == 0), stop=(c == 31))),
                                       waits if c == 0 else (), signal=(c == 31))
                    th = P.op("vector", (lambda e, bk=bk, xi=xi:
                                         e.scalar_tensor_tensor(out=hst[xi][:], in0=xt_t[xi][:], scalar=float(DN_ALPHA),
                                                                in1=pb[bk][:, :], op0=ALU.mult, op1=ALU.add)),
                              [last_mm, t_xl, h_free[xi]])
                    bank_free[bk] = th
                    x_free[xi] = th
                    h_free[xi] = P.dma("sync", h_s[tt * 128:(tt + 1) * 128, n8 * 512:(n8 + 1) * 512], hst[xi][:], s_h[xi], [th])
                wfree[b] = last_mm
            P.run_block()
        es_wo.close()

        with contextlib.ExitStack() as es:
            g_t = sb(es, "g_t", [128, D], F32)
            b_t = sb(es, "b_t", [128, D], F32)
            hb_ = [sb(es, "hb%d" % i, [128, D], F32) for i in range(3)]
            yb_ = [sb(es, "yb%d" % i, [128, D], F32) for i in range(3)]
            stats = [sb(es, "stats%d" % i, [128, 8, 6], F32) for i in range(3)]
            mv = [sb(es, "mv%d" % i, [128, 2], F32) for i in range(3)]
            rstd = [sb(es, "rstd%d" % i, [128, 1], F32) for i in range(3)]
            nmr = [sb(es, "nmr%d" % i, [128, 1], F32) for i in range(3)]
            eps_t = sb(es, "eps_t", [128, 1], F32)
            s_c = P.dsem(); s_hl = [P.dsem() for _ in range(3)]; s_yo = [P.dsem() for _ in range(3)]
            hfree = [None] * 3
            yfree = [None] * 3
            t_ls = {}
            t_ls[0] = P.dma("sync", hb_[0][:], h_s[0:128, :], s_hl[0])
            P.dma("sync", g_t[:], lng_bc[:], s_c)
            t_c = P.dma("sync", b_t[:], lnb_bc[:], s_c)
            t_ls[1] = P.dma("sync", hb_[1][:], h_s[128:256, :], s_hl[1])
            t_eps = P.op("vector", lambda e: e.memset(eps_t[:], float(LN_EPS)))
            small_free = [None] * 3
            for tt in range(10):
                i2 = tt % 3
                if tt + 2 < 10:
                    j2 = (tt + 2) % 3
                    t_ls[tt + 2] = P.dma("sync", hb_[j2][:], h_s[(tt + 2) * 128:(tt + 3) * 128, :], s_hl[j2], [hfree[j2]])
                t_l = t_ls[tt]
                t_s = None
                for k8 in range(8):
                    t_s = P.op("vector", (lambda e, i2=i2, k8=k8: e.bn_stats(out=stats[i2][:, k8, :], in_=hb_[i2][:, k8 * 512:(k8 + 1) * 512])),
                               [t_l, small_free[i2]])
                t_a = P.op("vector", (lambda e, i2=i2: e.bn_aggr(out=mv[i2][:], in_=stats[i2][:].rearrange("p a b -> p (a b)"))), [t_s])
                t_q = P.op("scalar", (lambda e, i2=i2: e.activation(out=rstd[i2][:], in_=mv[i2][:, 1:2], func=AF.Sqrt, bias=eps_t[:, 0:1], scale=1.0)),
                           [t_a, t_eps])
                t_r = P.op("vector", (lambda e, i2=i2: e.reciprocal(out=rstd[i2][:], in_=rstd[i2][:])), [t_q])
                t_m = P.op("vector", (lambda e, i2=i2: e.tensor_scalar(out=nmr[i2][:], in0=mv[i2][:, 0:1], scalar1=rstd[i2][:, 0:1], scalar2=-1.0,
                                                                        op0=ALU.mult, op1=ALU.mult)), [t_r])
                t_n = P.op("scalar", (lambda e, i2=i2: e.activation(out=yb_[i2][:], in_=hb_[i2][:], func=AF.Identity,
                                                                     bias=nmr[i2][:, 0:1], scale=rstd[i2][:, 0:1])), [t_m, yfree[i2]])
                hfree[i2] = t_n
                small_free[i2] = t_n
                t_g = P.op("vector", (lambda e, i2=i2: e.tensor_tensor(out=yb_[i2][:], in0=yb_[i2][:], in1=g_t[:], op=ALU.mult)), [t_n, t_c])
                t_b = P.op("gpsimd", (lambda e, i2=i2: e.tensor_tensor(out=yb_[i2][:], in0=yb_[i2][:], in1=b_t[:], op=ALU.add)), [t_g])
                yfree[i2] = P.dma("sync", y_out[tt * 128:(tt + 1) * 128, :], yb_[i2][:], s_yo[i2], [t_b])
            P.run_block()
    return nc


def _consts():
    c = np.zeros((128, C_END), np.float32)
    c[:, C_ID:C_ID + 128] = np.eye(128, dtype=np.float32)
    kp = np.arange(128)[:, None]
    k = np.arange(128)[None, :]
    c[:, C_TRI:C_TRI + 128] = np.where(kp >= k, -1.0, 0.0)
    c[:, C_ONE:C_ONE + 128] = -1.0
    q = np.arange(512)[None, :]
    for r in range(4):
        valid = (128 * r + kp) < q
        c[:, C_M01 + 512 * r:C_M01 + 512 * (r + 1)] = valid.astype(np.float32)
        c[:, C_MB + 512 * r:C_MB + 512 * (r + 1)] = np.where(valid, 0.0, -30000.0)
    qq = (np.arange(512) % 64)[None, :]
    valid = kp < qq
    c[:, C_SM01:C_SM01 + 512] = valid.astype(np.float32)
    c[:, C_SMB:C_SMB + 512] = np.where(valid, 0.0, -30000.0)
    return c


def _fm(v):
    return np.ascontiguousarray(v.reshape(-1, 128).T)


def make_in_maps(x_prompt, x_sample, cache_k, cache_v, state_conv, w_in, b_in, conv_w, conv_b,
                 w_a, w_b, w_o, ln_g, ln_b, cores=range(8)):
    f = np.float32
    w_in0 = np.ascontiguousarray(w_in[0], dtype=f)
    w_a0 = np.ascontiguousarray(w_a[0], dtype=f)
    w_b0 = np.ascontiguousarray(w_b[0], dtype=f)
    w_o0 = np.ascontiguousarray(w_o[0], dtype=f)
    bias_fm = _fm(np.asarray(b_in[0], f))
    bv_bc = np.ascontiguousarray(np.broadcast_to(np.asarray(b_in[0, 4096:6144], f)[None, :], (128, 2048)))
    cw = np.asarray(conv_w[0], f)
    cw_fm = np.ascontiguousarray(cw.reshape(3, 16, 128).transpose(2, 1, 0).reshape(128, 48))
    cb_fm = _fm(np.asarray(conv_b[0], f))
    lng = np.ascontiguousarray(np.broadcast_to(np.asarray(ln_g[0], f)[None, :], (128, D)))
    lnb = np.ascontiguousarray(np.broadcast_to(np.asarray(ln_b[0], f)[None, :], (128, D)))
    consts = _consts()
    maps = []
    for c in cores:
        bq, half = c // 2, c % 2
        xp = np.asarray(x_prompt[bq], f)
        own = xp[half * 1024:(half + 1) * 1024]
        xs = np.asarray(x_sample[4 * c:4 * c + 4], f).reshape(256, D)
        if half == 1:
            hist = xp[0:1024]
            flag = np.ones((128, 1), f)
        else:
            hist = np.zeros((1024, D), f)
            flag = np.zeros((128, 1), f)
        x_tok = np.ascontiguousarray(np.concatenate([own, xs], 0))
        x_ext = np.concatenate([own, xs, hist[1022:1024]], 0)
        xT_own = np.ascontiguousarray(x_ext.T.reshape(32, 128, NTX).transpose(1, 0, 2))
        xT_hist = np.ascontiguousarray(hist.T.reshape(32, 128, NHIST).transpose(1, 0, 2))
        ck = np.asarray(cache_k[0, 4 * c:4 * c + 4], f)
        ckT = np.ascontiguousarray(ck.transpose(0, 2, 3, 1))
        cv = np.ascontiguousarray(np.asarray(cache_v[0, 4 * c:4 * c + 4], f).reshape(4, PAST, 2048))
        sc = np.asarray(state_conv[0, 4 * c:4 * c + 4], f)
        sc_fm = np.ascontiguousarray(sc.reshape(4, 2, 16, 128).transpose(3, 2, 0, 1).reshape(128, 128))
        maps.append(dict(xT_own=xT_own, xT_hist=xT_hist, x_tok=x_tok, w_in=w_in0, w_a=w_a0, w_b=w_b0, w_o=w_o0,
                         bias_fm=bias_fm, bv_bc=bv_bc, cw_fm=cw_fm, cb_fm=cb_fm, sc_fm=sc_fm, lng_bc=lng, lnb_bc=lnb,
                         flag=flag, consts=consts, ckT=ckT, cv=cv))
    return maps


_NC_CACHE = {}


def kernel(x_prompt, x_sample, cache_k, cache_v, state_conv, w_in, b_in, conv_w, conv_b,
           w_a, w_b, w_o, ln_g, ln_b):
    args = [np.asarray(a) for a in (x_prompt, x_sample, cache_k, cache_v, state_conv, w_in, b_in, conv_w, conv_b,
                                    w_a, w_b, w_o, ln_g, ln_b)]
    maps = make_in_maps(*args)
    if "nc" not in _NC_CACHE:
        _NC_CACHE["nc"] = build_program(False)
    nc = _NC_CACHE["nc"]
    res = run_bass_kernel_spmd(nc, maps, core_ids=list(range(8)))
    R = res.results
    f = np.float32
    y_prompt = np.zeros((4, 2048, D), f)
    y_sample = np.zeros((32, 64, D), f)
    k_prompt = np.zeros((1, 4, 2048, NH, HD), f)
    v_prompt = np.zeros((1, 4, 2048, NH, HD), f)
    conv_prompt = np.zeros((1, 4, 2, 2048), f)
    k_new = np.zeros((1, 32, 64, NH, HD), f)
    v_new = np.zeros((1, 32, 64, NH, HD), f)
    conv_sample = np.zeros((1, 32, 2, 2048), f)
    for c in range(8):
        bq, half = c // 2, c % 2
        r = R[c]
        y = np.asarray(r["y_out"]); k = np.asarray(r["k_out"]); v = np.asarray(r["v_out"])
        co = np.asarray(r["conv_out"]).reshape(128, 16, 10)
        sl = slice(half * 1024, (half + 1) * 1024)
        y_prompt[bq, sl] = y[0:1024]
        y_sample[4 * c:4 * c + 4] = y[1024:].reshape(4, 64, D)
        k_prompt[0, bq, sl] = k[0:1024].reshape(1024, NH, HD)
        v_prompt[0, bq, sl] = v[0:1024].reshape(1024, NH, HD)
        k_new[0, 4 * c:4 * c + 4] = k[1024:].reshape(4, 64, NH, HD)
        v_new[0, 4 * c:4 * c + 4] = v[1024:].reshape(4, 64, NH, HD)
        ch = co.transpose(2, 1, 0).reshape(10, 2048)
        if half == 1:
            conv_prompt[0, bq] = ch[0:2]
        conv_sample[0, 4 * c:4 * c + 4] = ch[2:10].reshape(4, 2, 2048)
    return (y_prompt, y_sample, k_prompt, v_prompt, conv_prompt, k_new, v_new, conv_sample)
```
